# Optimizing a Trainium2 kernel written in Bass

```python
import math
import jax, jax.numpy as jnp
from jax import lax
import numpy as np

D_MODEL = 1024
BATCH = 8
SEQ = 4096
DEPTH = 2

SSM_HEADS = 16
SSM_HEAD_DIM = 64
SSM_D_INNER = SSM_HEADS * SSM_HEAD_DIM
SSM_GROUPS = 2
SSM_STATE = 128
SSM_CONV_DIM = SSM_D_INNER + 2 * SSM_GROUPS * SSM_STATE
GDN_HEADS = 8
GDN_HEAD_K = 128
GDN_HEAD_V = 128
GDN_K_DIM = GDN_HEADS * GDN_HEAD_K
GDN_V_DIM = GDN_HEADS * GDN_HEAD_V
GDN_QKV_DIM = 2 * GDN_K_DIM + GDN_V_DIM
CONV_K = 4
CHUNK = 64
FFN_HIDDEN = -(-8 * D_MODEL // (3 * 256)) * 256
EPS = 1e-6
IN_SPLIT_SIZES = (SSM_D_INNER, SSM_CONV_DIM, SSM_HEADS,
                  GDN_QKV_DIM, GDN_V_DIM, GDN_HEADS, GDN_HEADS,
                  D_MODEL, D_MODEL)
IN_DIM = sum(IN_SPLIT_SIZES)
IN_SPLIT_IDX = tuple(int(i) for i in np.cumsum(IN_SPLIT_SIZES)[:-1])

kernel_name = "hybrid_ssd_gdn_gated_merge_block"


def rmsnorm(x, w):
    xf = x.astype(jnp.float32)
    y = xf * lax.rsqrt(jnp.mean(xf * xf, axis=-1, keepdims=True) + EPS)
    return (y * w.astype(jnp.float32)).astype(x.dtype)


def causal_dwconv(u, w):
    c = u.shape[-1]
    return lax.conv_general_dilated(
        u, w[:, None, :].astype(u.dtype), window_strides=(1,),
        padding=[(CONV_K - 1, 0)], dimension_numbers=("NWC", "WIO", "NWC"),
        feature_group_count=c)


def to_chunks(t):
    b, s = t.shape[:2]
    return jnp.moveaxis(t.reshape(b, s // CHUNK, CHUNK, *t.shape[2:]), 1, 0)


def from_chunks(t):
    t = jnp.moveaxis(t, 0, 1)
    return t.reshape(t.shape[0], t.shape[1] * t.shape[2], *t.shape[3:])


def ssd_scan(xdt, a, bm, cm):
    bsz, s = xdt.shape[:2]
    r = SSM_HEADS // SSM_GROUPS
    xdt = xdt.reshape(bsz, s, SSM_GROUPS, r, SSM_HEAD_DIM)
    a = a.reshape(bsz, s, SSM_GROUPS, r)
    causal = jnp.tril(jnp.ones((CHUNK, CHUNK), dtype=bool))[None, :, :, None, None]

    def body(state, inp):
        xc, ac, bc, cc = inp
        acum = jnp.cumsum(ac, axis=1)
        seg = acum[:, :, None] - acum[:, None, :]
        lmat = jnp.exp(jnp.where(causal, seg, -jnp.inf))
        cb = jnp.einsum("bign,bjgn->bijg", cc, bc)
        y_diag = jnp.einsum("bijg,bijgr,bjgrp->bigrp", cb, lmat, xc)
        y_off = jnp.einsum("bign,bgrpn->bigrp", cc, state) * jnp.exp(acum)[..., None]
        decay_end = jnp.exp(acum[:, -1:] - acum)
        new_state = state * jnp.exp(acum[:, -1])[..., None, None] + jnp.einsum(
            "bjgn,bjgr,bjgrp->bgrpn", bc, decay_end, xc)
        return new_state, y_diag + y_off

    state0 = jnp.zeros((bsz, SSM_GROUPS, r, SSM_HEAD_DIM, SSM_STATE), jnp.float32)
    _, y = lax.scan(body, state0, (to_chunks(xdt), to_chunks(a), to_chunks(bm), to_chunks(cm)))
    return from_chunks(y).reshape(bsz, s, SSM_HEADS, SSM_HEAD_DIM)


def mamba2_mixer(z, xbc, dt_raw, conv_w, conv_b, dt_bias, a_log, d_skip, norm_w):
    dtype = z.dtype
    bsz, s = z.shape[:2]
    xbc = jax.nn.silu(causal_dwconv(xbc, conv_w) + conv_b.astype(xbc.dtype))
    xs, bm, cm = jnp.split(xbc.astype(jnp.float32),
                           [SSM_D_INNER, SSM_D_INNER + SSM_GROUPS * SSM_STATE], axis=-1)
    xs = xs.reshape(bsz, s, SSM_HEADS, SSM_HEAD_DIM)
    bm = bm.reshape(bsz, s, SSM_GROUPS, SSM_STATE)
    cm = cm.reshape(bsz, s, SSM_GROUPS, SSM_STATE)
    dt = jax.nn.softplus(dt_raw.astype(jnp.float32) + dt_bias.astype(jnp.float32))
    a = -jnp.exp(a_log.astype(jnp.float32))
    y = ssd_scan(xs * dt[..., None], dt * a, bm, cm)
    y = y + d_skip.astype(jnp.float32)[:, None] * xs
    y = y.reshape(bsz, s, SSM_D_INNER) * jax.nn.silu(z.astype(jnp.float32))
    y = y.reshape(bsz, s, SSM_GROUPS, SSM_D_INNER // SSM_GROUPS)
    y = y * lax.rsqrt(jnp.mean(y * y, axis=-1, keepdims=True) + EPS)
    y = y.reshape(bsz, s, SSM_D_INNER) * norm_w.astype(jnp.float32)
    return y.astype(dtype)


def gdn_scan(q, k, v, g, beta):
    bsz = q.shape[0]
    tril = jnp.tril(jnp.ones((CHUNK, CHUNK), dtype=bool))
    strict = jnp.tril(jnp.ones((CHUNK, CHUNK), dtype=bool), -1)
    eye = jnp.eye(CHUNK, dtype=jnp.float32)

    def body(state, inp):
        qc, kc, vc, gcv, bc = inp
        gc = jnp.cumsum(gcv, axis=1)
        gch = jnp.swapaxes(gc, 1, 2)
        decay = jnp.exp(jnp.where(tril, gch[..., :, None] - gch[..., None, :], -jnp.inf))
        kk = jnp.einsum("bihd,bjhd->bhij", kc, kc)
        amat = jnp.where(strict, kk * decay * jnp.swapaxes(bc, 1, 2)[..., :, None], 0.0)
        rhs = jnp.concatenate([vc * bc[..., None], kc * (bc * jnp.exp(gc))[..., None]], axis=-1)
        rhs = jnp.swapaxes(rhs, 1, 2)
        t = lax.linalg.triangular_solve(eye + amat, rhs, left_side=True, lower=True,
                                        unit_diagonal=True)
        u, w = t[..., :GDN_HEAD_V], t[..., GDN_HEAD_V:]
        v_new = u - jnp.einsum("bhqk,bhkv->bhqv", w, state)
        qk = jnp.einsum("bihd,bjhd->bhij", qc, kc) * decay
        o = (jnp.einsum("bihk,bhkv->bhiv", qc * jnp.exp(gc)[..., None], state)
             + jnp.einsum("bhij,bhjv->bhiv", qk, v_new))
        new_state = state * jnp.exp(gch[..., -1])[..., None, None] + jnp.einsum(
            "bjhk,bhjv->bhkv", kc * jnp.exp(gc[:, -1:] - gc)[..., None], v_new)
        return new_state, jnp.swapaxes(o, 1, 2)

    state0 = jnp.zeros((bsz, GDN_HEADS, GDN_HEAD_K, GDN_HEAD_V), jnp.float32)
    _, o = lax.scan(body, state0, (to_chunks(q), to_chunks(k), to_chunks(v),
                                   to_chunks(g), to_chunks(beta)))
    return from_chunks(o)


def gdn_mixer(qkv, z, a_raw, b_raw, conv_w, a_log, dt_bias, norm_w):
    dtype = z.dtype
    bsz, s = z.shape[:2]
    qkv = jax.nn.silu(causal_dwconv(qkv, conv_w)).astype(jnp.float32)
    q, k, v = jnp.split(qkv, [GDN_K_DIM, 2 * GDN_K_DIM], axis=-1)
    q = q.reshape(bsz, s, GDN_HEADS, GDN_HEAD_K)
    k = k.reshape(bsz, s, GDN_HEADS, GDN_HEAD_K)
    v = v.reshape(bsz, s, GDN_HEADS, GDN_HEAD_V)
    q = q * lax.rsqrt(jnp.sum(q * q, axis=-1, keepdims=True) + EPS) * (GDN_HEAD_K ** -0.5)
    k = k * lax.rsqrt(jnp.sum(k * k, axis=-1, keepdims=True) + EPS)
    beta = jax.nn.sigmoid(b_raw.astype(jnp.float32))
    g = -jnp.exp(a_log.astype(jnp.float32)) * jax.nn.softplus(
        a_raw.astype(jnp.float32) + dt_bias.astype(jnp.float32))
    o = gdn_scan(q, k, v, g, beta)
    o = o * lax.rsqrt(jnp.mean(o * o, axis=-1, keepdims=True) + EPS) * norm_w.astype(jnp.float32)
    o = o * jax.nn.silu(z.astype(jnp.float32).reshape(bsz, s, GDN_HEADS, GDN_HEAD_V))
    return o.reshape(bsz, s, GDN_V_DIM).astype(dtype)


def setup_inputs(seed: int = 0) -> dict:
    key = jax.random.key(seed)
    ks = jax.random.split(key, 24)
    f32 = jnp.float32

    def nrm(k, shape, scale):
        return jax.random.normal(k, shape, f32) * scale

    def gain(k, shape):
        return 1.0 + 0.02 * jax.random.normal(k, shape, f32)

    def dt_bias_init(k, shape):
        dt = jnp.exp(jax.random.uniform(k, shape, f32, math.log(1e-3), math.log(1e-1)))
        return dt + jnp.log(-jnp.expm1(-dt))

    return {
        "x": jax.random.normal(ks[0], (BATCH, SEQ, D_MODEL), f32),
        "norm_mix_w": gain(ks[1], (DEPTH, D_MODEL)),
        "w_in": nrm(ks[2], (DEPTH, D_MODEL, IN_DIM), D_MODEL ** -0.5),
        "ssm_conv_w": nrm(ks[3], (DEPTH, CONV_K, SSM_CONV_DIM), CONV_K ** -0.5),
        "ssm_conv_b": nrm(ks[4], (DEPTH, SSM_CONV_DIM), 0.02),
        "ssm_dt_bias": dt_bias_init(ks[5], (DEPTH, SSM_HEADS)),
        "ssm_a_log": jnp.log(jax.random.uniform(ks[6], (DEPTH, SSM_HEADS), f32, 1.0, 16.0)),
        "ssm_d": gain(ks[7], (DEPTH, SSM_HEADS)),
        "ssm_norm_w": gain(ks[8], (DEPTH, SSM_D_INNER)),
        "gdn_conv_w": nrm(ks[9], (DEPTH, CONV_K, GDN_QKV_DIM), CONV_K ** -0.5),
        "gdn_a_log": jnp.log(jax.random.uniform(ks[10], (DEPTH, GDN_HEADS), f32, 1.0, 16.0)),
        "gdn_dt_bias": dt_bias_init(ks[11], (DEPTH, GDN_HEADS)),
        "gdn_norm_w": gain(ks[12], (DEPTH, GDN_HEAD_V)),
        "w_proj_ssm": nrm(ks[13], (DEPTH, SSM_D_INNER, D_MODEL), SSM_D_INNER ** -0.5),
        "w_proj_gdn": nrm(ks[14], (DEPTH, GDN_V_DIM, D_MODEL), GDN_V_DIM ** -0.5),
        "w_out": nrm(ks[15], (DEPTH, D_MODEL, D_MODEL), D_MODEL ** -0.5),
        "norm_ffn_w": gain(ks[16], (DEPTH, D_MODEL)),
        "w_ffn_in": nrm(ks[17], (DEPTH, D_MODEL, 2 * FFN_HIDDEN), D_MODEL ** -0.5),
        "w_ffn_down": nrm(ks[18], (DEPTH, FFN_HIDDEN, D_MODEL), FFN_HIDDEN ** -0.5),
        "final_norm_w": gain(ks[19], (D_MODEL,)),
    }


def reference(x, norm_mix_w, w_in, ssm_conv_w, ssm_conv_b, ssm_dt_bias, ssm_a_log, ssm_d,
              ssm_norm_w, gdn_conv_w, gdn_a_log, gdn_dt_bias, gdn_norm_w, w_proj_ssm,
              w_proj_gdn, w_out, norm_ffn_w, w_ffn_in, w_ffn_down, final_norm_w):
    for l in range(DEPTH):
        h = rmsnorm(x, norm_mix_w[l])
        proj = h @ w_in[l]
        (ssm_z, ssm_xbc, ssm_dt, gdn_qkv, gdn_z, gdn_a, gdn_b,
         gate_ssm, gate_gdn) = jnp.split(proj, IN_SPLIT_IDX, axis=-1)
        y_ssm = mamba2_mixer(ssm_z, ssm_xbc, ssm_dt, ssm_conv_w[l], ssm_conv_b[l],
                             ssm_dt_bias[l], ssm_a_log[l], ssm_d[l], ssm_norm_w[l])
        y_gdn = gdn_mixer(gdn_qkv, gdn_z, gdn_a, gdn_b, gdn_conv_w[l], gdn_a_log[l],
                          gdn_dt_bias[l], gdn_norm_w[l])
        merged = (jax.nn.sigmoid(gate_ssm) * (y_ssm @ w_proj_ssm[l])
                  + jax.nn.sigmoid(gate_gdn) * (y_gdn @ w_proj_gdn[l]))
        x = x + merged @ w_out[l]
        h = rmsnorm(x, norm_ffn_w[l])
        gate, up = jnp.split(h @ w_ffn_in[l], [FFN_HIDDEN], axis=-1)
        x = x + (jax.nn.silu(gate) * up) @ w_ffn_down[l]
    return rmsnorm(x, final_norm_w)
```

```python
import contextlib
import os
import numpy as np
import concourse.bass as bass
import concourse.mybir as mybir
from concourse.bass_utils import run_bass_kernel_spmd

F32 = mybir.dt.float32
F32R = mybir.dt.float32
BF16 = mybir.dt.bfloat16
AF = mybir.ActivationFunctionType
ALU = mybir.AluOpType
AX = mybir.AxisListType

D = 1024
SEQ = 4096
BATCH = 8
DEPTH = 2
IN_DIM = 8736
FFN_H = 2816
TB = 256
NT = TB // 128
EPS = 1e-6
PPL = 181
O_NMIX, O_NFFN, O_SNW, O_GNW, O_SCW, O_SCB, O_GCW = 0, 8, 16, 24, 25, 73, 85
BPL = 64
NCH = 42


class _Op:
    __slots__ = ("eng", "fn", "reads", "writes", "dsem", "gfinal", "deps", "signal", "tick", "idx")


class Prog:
    ENGINES = ("pe", "act", "dve", "pool", "sp")

    def __init__(self, nc):
        self.nc = nc
        self.ops = []

    def add(self, eng, fn, reads=(), writes=(), dsem=None, gfinal=False):
        op = _Op()
        op.eng, op.fn, op.reads, op.writes, op.dsem, op.gfinal = eng, fn, tuple(reads), tuple(writes), dsem, gfinal
        op.deps, op.signal, op.tick = [], False, None
        op.idx = len(self.ops)
        self.ops.append(op)
        return op

    def dma(self, out, in_, reads, writes, dsem, eng="sp", gfinal=False):
        return self.add(eng, lambda e: e.dma_start(out=out, in_=in_), reads, writes, dsem, gfinal)

    def emit(self, final_wait_ops=()):
        nc = self.nc
        ops = self.ops
        last_w = {}
        readers = {}
        for op in ops:
            deps = {}
            for t in op.reads:
                w = last_w.get(t)
                if w is not None:
                    deps[w.idx] = w
            for t in op.writes:
                w = last_w.get(t)
                if w is not None and not (op.eng == "pe" and w.eng == "pe" and isinstance(t, tuple)
                                          and t[0] == "ps"):
                    deps[w.idx] = w
                for r in readers.get(t, ()):
                    deps[r.idx] = r
            deps.pop(op.idx, None)
            op.deps = list(deps.values())
            for t in op.reads:
                readers.setdefault(t, []).append(op)
            for t in op.writes:
                last_w[t] = op
                readers[t] = []
        for op in ops:
            for d in op.deps:
                d.signal = True
        for op in final_wait_ops:
            op.signal = True
        counts = {}
        for op in ops:
            if op.dsem is not None:
                k = ("dma", op.dsem)
                counts[k] = counts.get(k, 0) + 16
                op.tick = (k, counts[k])
            elif op.signal:
                k = ("eng", op.eng)
                c = counts.get(k, 0) + 1
                counts[k] = c
                op.tick = (k + ((c - 1) // 20000,), (c - 1) % 20000 + 1)
        for op in ops:
            if op.dsem is not None and op.gfinal:
                op.tick = (op.tick[0], counts[op.tick[0]])
        keys = []
        for op in ops:
            if op.tick is not None and op.tick[0] not in keys:
                keys.append(op.tick[0])
        with contextlib.ExitStack() as st:
            sems = {}
            for i, k in enumerate(keys):
                sems[k] = st.enter_context(nc.semaphore("sem%d" % i))
            block = st.enter_context(nc.Block())
            per_eng = {e: [o for o in ops if o.eng == e] for e in self.ENGINES}
            final = list(final_wait_ops)

            def run_stream(eng_name, e):
                known = {}
                for op in per_eng[eng_name]:
                    need = {}
                    for d in op.deps:
                        k, v = d.tick
                        if known.get(k, -1) >= v:
                            continue
                        if need.get(k, -1) < v:
                            need[k] = v
                    for k, v in need.items():
                        e.wait_ge(sems[k], v)
                        known[k] = v
                    ins = op.fn(e)
                    if op.tick is not None:
                        k, v = op.tick
                        ins.then_inc(sems[k], 16 if k[0] == "dma" else 1)
                if eng_name == "sp":
                    for op in final:
                        k, v = op.tick
                        if known.get(k, -1) < v:
                            e.wait_ge(sems[k], v)
                            known[k] = v

            @block.sync
            def _(e):
                run_stream("sp", e)

            @block.tensor
            def _(e):
                run_stream("pe", e)

            @block.scalar
            def _(e):
                run_stream("act", e)

            @block.vector
            def _(e):
                run_stream("dve", e)

            @block.gpsimd
            def _(e):
                run_stream("pool", e)


def chunk_specs():
    sp = []
    for c in range(2):
        sp.append(("w_in", 0, 1024, c * 512, 512))
    for c in range(3):
        sp.append(("w_in", 0, 1024, 1024 + c * 512, 512))
    sp.append(("small", 0, 1024, 0, 32))
    for c in range(6):
        sp.append(("w_in", 0, 1024, 2576 + c * 512, 512))
    for c in range(2):
        sp.append(("w_in", 0, 1024, 5648 + c * 512, 512))
    for c in range(4):
        sp.append(("w_in", 0, 1024, 6688 + c * 512, 512))
    for nm in ("w_proj_ssm", "w_proj_gdn", "w_out"):
        for c in range(2):
            sp.append((nm, 0, 1024, c * 512, 512))
    for q in range(6):
        w = min(512, FFN_H - q * 512)
        sp.append(("w_ffn_in", 0, 1024, q * 512, w))
        sp.append(("w_ffn_in", 0, 1024, FFN_H + q * 512, w))
    for c in range(2):
        for kg in range(3):
            r0 = kg * 1024
            sp.append(("w_ffn_down", r0, min(1024, FFN_H - r0), c * 512, 512))
    assert len(sp) == NCH
    return sp


USE_ORDER = ([0, 1, 2, 3, 4, 5] + list(range(6, 14)) + [14, 15, 16, 17, 18, 20, 19, 21, 22, 23]
             + list(range(24, 42)))


def build(NB=SEQ // TB, depth=DEPTH, dbg=(), stage=100, dbgsb=False):
    nc = bass.Bass("TRN2", target_bir_lowering=False)
    S = NB * TB
    x_in = nc.dram_tensor("x", [S, D], F32, kind="ExternalInput").ap()
    wd = {}
    wd["w_in"] = nc.dram_tensor("w_in", [depth, D, IN_DIM], F32, kind="ExternalInput").ap()
    wd["w_proj_ssm"] = nc.dram_tensor("w_proj_ssm", [depth, D, D], F32, kind="ExternalInput").ap()
    wd["w_proj_gdn"] = nc.dram_tensor("w_proj_gdn", [depth, D, D], F32, kind="ExternalInput").ap()
    wd["w_out"] = nc.dram_tensor("w_out", [depth, D, D], F32, kind="ExternalInput").ap()
    wd["w_ffn_in"] = nc.dram_tensor("w_ffn_in", [depth, D, 2 * FFN_H], F32, kind="ExternalInput").ap()
    wd["w_ffn_down"] = nc.dram_tensor("w_ffn_down", [depth, FFN_H, D], F32, kind="ExternalInput").ap()
    pp_d = nc.dram_tensor("pp", [128, depth * PPL], F32, kind="ExternalInput").ap()
    bp_d = nc.dram_tensor("bp", [128, depth * BPL + D], F32, kind="ExternalInput").ap()
    cst_d = nc.dram_tensor("cst", [128, 5 * 128], F32, kind="ExternalInput").ap()
    out_d = nc.dram_tensor("out", [S, D], F32, kind="ExternalOutput").ap()
    wscr = nc.dram_tensor("wscr", [depth * NCH, 128, 8 * 512], BF16, kind="Internal").ap()
    dbg_d = {}
    for nm, shp in dbg:
        dbg_d[nm] = nc.dram_tensor("dbg_" + nm, list(shp), F32, kind="ExternalOutput").ap()

    specs = chunk_specs()
    P = Prog(nc)
    with contextlib.ExitStack() as st:
        def sb(name, shape, dt):
            return st.enter_context(nc.sbuf_tensor("s_" + name, list(shape), dt))

        ps = [st.enter_context(nc.psum_tensor("ps%d" % i, [128, 512], F32)) for i in range(8)]

        def pk(b, qs=None):
            return [("ps", b)]

        rrA = [0]
        rrB = [0]

        def bankA():
            b = rrA[0] % 4
            rrA[0] += 1
            return b

        def bankB():
            b = 4 + rrB[0] % 4
            rrB[0] += 1
            return b

        cst = sb("cst", [128, 5 * 128], F32)
        ident_f = cst[:, 0:128]
        LOWS = cst[:, 128:256]
        UPI = cst[:, 256:384]
        UPS = cst[:, 384:512]
        ones_f = cst[:, 512:640]
        cstb = sb("cstb", [128, 2 * 128], BF16)
        ident_b = cstb[:, 0:128]
        ones_b = cstb[:, 128:256]
        pp = sb("pp", [128, depth * PPL], F32)
        bp = sb("bp", [128, depth * BPL + D], F32)
        nexpA = sb("nexpA", [128, depth * 24], F32)
        epsT = sb("epsT", [128, 1], F32)
        xb = sb("xb", [128, NT, D], F32)
        hT = sb("hT", [128, 8, TB], BF16)
        wb = [sb("wb%d" % i, [128, 8, 512], BF16) for i in range(3)]
        zs = sb("zs", [128, NT, D], BF16)
        zg = sb("zg", [128, NT, D], BF16)
        gT = sb("gT", [128, 16, TB], BF16)
        xs_tok = sb("xs_tok", [128, NT, D], BF16)
        B_tok = sb("B_tok", [128, NT, 256], BF16)
        BT = sb("BT", [128, 2, TB], BF16)
        CT = sb("CT", [128, 2, TB], BF16)
        qT = sb("qT", [128, 8, TB], BF16)
        kT = sb("kT", [128, 8, TB], BF16)
        k_tok = sb("k_tok", [128, NT, D], BF16)
        v_tok = sb("v_tok", [128, NT, D], BF16)
        yT = sb("yT", [128, 8, TB], BF16)
        oT = sb("oT", [128, 8, TB], BF16)
        actT = sb("actT", [128, 22, TB], BF16)
        sm_tok = sb("sm_tok", [128, NT, 64], F32)
        carry = sb("carry", [128, depth * 36, 3], F32)
        ubuf = [sb("ubuf%d" % i, [128, TB + 3], F32) for i in range(2)]
        cacc = [sb("cacc%d" % i, [128, TB], F32) for i in range(2)]
        ftmp = [sb("ftmp%d" % i, [128, TB], BF16) for i in range(2)]
        sqb = sb("sqb", [128, TB], BF16)
        rq = sb("rq", [128, TB], F32)
        junk = sb("junk", [128, D], BF16)
        junk32 = sb("junk32", [128, D], F32)
        ssq = sb("ssq", [128, 16], F32)
        lnv = sb("lnv", [128, 16], F32)
        rstd = sb("rstd", [128, 16], F32)
        xn = sb("xn", [128, D], BF16)
        Gm = sb("Gm", [128, 8, 128], F32)
        LT = sb("LT", [128, 8, 128], F32)
        LsT = sb("LsT", [128, 8, 128], F32)
        ea = sb("ea", [128, 48], F32)
        eg = sb("eg", [128, 24], F32)
        CBm = sb("CBm", [128, 128], F32)
        MT = sb("MT", [128, 8, 128], BF16)
        xdt = sb("xdt", [128, 16, 64], BF16)
        xdtd = sb("xdtd", [128, 16, 64], BF16)
        tokf = sb("tokf", [128, D], F32)
        xsD = sb("xsD", [128, D], F32)
        tokb = sb("tokb", [128, D], BF16)
        smt = sb("smt", [128, 64], F32)
        mtmp = [sb("mtmp%d" % i, [128, 2, TB], F32) for i in range(2)]
        Xm = sb("Xm", [128, 8, 128], BF16)
        XTm = sb("XTm", [128, 8, 128], BF16)
        PTm = sb("PTm", [128, 8, 128], F32)
        PTb = sb("PTb", [128, 8, 128], BF16)
        qkmT = sb("qkmT", [128, 8, 128], BF16)
        Rk = sb("Rk", [128, 8, 128], BF16)
        kdec = sb("kdec", [128, 8, 128], BF16)
        nZkT = sb("nZkT", [128, 8, 128], BF16)
        vnew = sb("vnew", [128, 8, 128], BF16)
        Ss = [sb("Ss%d" % l, [128, D], F32) for l in range(depth)]
        SsB = [sb("SsB%d" % l, [128, D], BF16) for l in range(depth)]
        Sg = [sb("Sg%d" % l, [128, 8, 128], F32) for l in range(depth)]
        SgB = [sb("SgB%d" % l, [128, 8, 128], BF16) for l in range(depth)]

        def mm(out_ap, pairs, reads, writes):
            def fn(e):
                n = len(pairs)
                ins = None
                for i, (l_, r_) in enumerate(pairs):
                    ins = e.matmul(out_ap, l_, r_, start=(i == 0), stop=(i == n - 1))
                return ins
            return P.add("pe", fn, reads, writes)

        def tr(out_ap, in_ap, idn, reads, writes):
            return P.add("pe", lambda e: e.transpose(out_ap, in_ap, idn), reads, writes)

        def actf(out, in_, func, reads, writes, scale=1.0, bias=0.0, accum=None):
            def fn(e):
                kw = {}
                if accum is not None:
                    kw["accum_out"] = accum
                return e.activation(out=out, in_=in_, func=func, bias=bias, scale=scale, **kw)
            return P.add("act", fn, reads, writes)

        def tt(eng, out, in0, in1, op, reads, writes):
            return P.add(eng, lambda e: e.tensor_tensor(out=out, in0=in0, in1=in1, op=op), reads, writes)

        def ts(eng, out, in0, s1, op0, reads, writes, s2=None, op1=None):
            if op1 is None:
                return P.add(eng, lambda e: e.tensor_scalar(out=out, in0=in0, scalar1=s1, scalar2=None, op0=op0),
                             reads, writes)
            return P.add(eng, lambda e: e.tensor_scalar(out=out, in0=in0, scalar1=s1, scalar2=s2, op0=op0, op1=op1),
                         reads, writes)

        def stt(out, in0, scalar, in1, op0, op1, reads, writes):
            return P.add("dve", lambda e: e.scalar_tensor_tensor(out=out, in0=in0, scalar=scalar, in1=in1,
                                                                  op0=op0, op1=op1), reads, writes)

        def cp(eng, out, in_, reads, writes):
            return P.add(eng, lambda e: e.tensor_copy(out=out, in_=in_), reads, writes)

        def rstd_from(ssum_ap, n, inv_count, key):
            actf(lnv[:, 0:n], ssum_ap, AF.Ln, [key, "epsT"], ["lnv"], scale=inv_count, bias=epsT[:, 0:1])
            actf(rstd[:, 0:n], lnv[:, 0:n], AF.Exp, ["lnv"], ["rstd"], scale=-0.5)

        def bc(ap, shape):
            return ap.to_broadcast(list(shape))

        P.dma(cst[:, :], cst_d, [], ["cst"], "par", gfinal=True)
        P.dma(pp[:, :], pp_d, [], ["pp"], "par", gfinal=True)
        P.dma(bp[:, :], bp_d, [], ["bp"], "par", gfinal=True)
        cp("dve", ident_b, ident_f, ["cst"], ["cstb"])
        cp("dve", ones_b, ones_f, ["cst"], ["cstb"])
        P.add("dve", lambda e: e.memset(epsT[:, :], EPS), [], ["epsT"])
        P.add("dve", lambda e: e.memset(carry[:, :, :], 0.0), [], ["carry"])
        P.add("dve", lambda e: e.memset(sm_tok[:, :, :], 0.0), [], [("sm", t) for t in range(NT)])
        for l in range(depth):
            P.add("dve", lambda e, l=l: e.memset(Ss[l][:, :], 0.0), [], [("Ss", l)])
            P.add("dve", lambda e, l=l: e.memset(SsB[l][:, :], 0.0), [], [("SsB", l)])
            P.add("dve", lambda e, l=l: e.memset(Sg[l][:, :, :], 0.0), [], [("Sg", l)])
            P.add("dve", lambda e, l=l: e.memset(SgB[l][:, :, :], 0.0), [], [("SgB", l)])
            o = l * BPL
            actf(nexpA[:, l * 24:l * 24 + 16], bp[:, o + 16:o + 32], AF.Exp, ["bp"], ["nexpA_t"])
            actf(nexpA[:, l * 24 + 16:l * 24 + 24], bp[:, o + 48:o + 56], AF.Exp, ["bp"], ["nexpA_t"])
        ts("dve", nexpA[:, :], nexpA[:, :], -1.0, ALU.mult, ["nexpA_t"], ["nexpA"])

        ncv = [0]
        for l in range(depth):
            for cid in USE_ORDER:
                nm, r0, nr, c0, ncol = specs[cid]
                dst = wscr[l * NCH + cid].rearrange("p (k n) -> p k n", k=8)
                key = ("wscr", l, cid)
                slot = ncv[0] % 8
                ncv[0] += 1
                ds = "cv%d" % slot
                sk = ("cvslot", slot)
                if nm == "small":
                    for (cs, dcol, w) in ((2560, 0, 16), (6672, 16, 16)):
                        P.dma(dst[:, :, dcol:dcol + w],
                              wd["w_in"][l, :, cs:cs + w].rearrange("(k p) n -> p k n", p=128),
                              [], [key, sk], ds, eng="pool")
                else:
                    kc = nr // 128
                    P.dma(dst[:, 0:kc, 0:ncol],
                          wd[nm][l, r0:r0 + nr, c0:c0 + ncol].rearrange("(k p) n -> p k n", p=128),
                          [], [key, sk], ds, eng="pool")

        order = []
        n_use = {0: 0, 1: 2, 2: 5, 3: 6, 4: 6, 5: 14, 6: 14, 7: 22}.get(stage, 24 if stage < 100 else 42)
        for blk in range(NB):
            for l in range(depth):
                for cid in USE_ORDER[:n_use]:
                    order.append((l, cid))
        wstate = {"pos": 0, "issued": 0}

        def wuse(k=1, expect=None):
            n = wstate["pos"]
            lim = min(n + 3, len(order))
            while wstate["issued"] < lim:
                i = wstate["issued"]
                l_, cid = order[i]
                _nm, _r0, _nr, _c0, _ncol = specs[cid]
                _kc = _nr // 128
                P.dma(wb[i % 3][:, 0:_kc, 0:_ncol],
                      wscr[l_ * NCH + cid].rearrange("p (k n) -> p k n", k=8)[:, 0:_kc, 0:_ncol],
                      [("wscr", l_, cid)], [("wb", i % 3)], "wb%d" % (i % 3))
                wstate["issued"] += 1
            res = []
            for j in range(k):
                if expect is not None:
                    assert order[n + j] == expect[j], (order[n + j], expect[j])
                res.append(((n + j) % 3))
            wstate["pos"] += k
            return res

        def rmsnorm_T(l, off):
            for t in range(NT):
                actf(junk[:, :], xb[:, t, :], AF.Square, [("xb", t)], ["junk", ("ssq", t)], accum=ssq[:, t:t + 1])
            actf(lnv[:, 0:NT], ssq[:, 0:NT], AF.Ln, [("ssq", t) for t in range(NT)] + ["epsT"], ["lnv"],
                 scale=1.0 / D, bias=epsT[:, 0:1])
            actf(rstd[:, 0:NT], lnv[:, 0:NT], AF.Exp, ["lnv"], ["rstd"], scale=-0.5)
            for t in range(NT):
                ts("dve", xn[:, :], xb[:, t, :], rstd[:, t:t + 1], ALU.mult, [("xb", t), "rstd"], ["xn"])
                b = bankB()
                pv = ps[b][:, :].bitcast(BF16)
                for kc in range(8):
                    tr(pv[:, kc * 128:(kc + 1) * 128], xn[:, kc * 128:(kc + 1) * 128], ident_b,
                       ["xn", "cstb"], pk(b, [kc // 2]))
                wcol = pp[:, l * PPL + off:l * PPL + off + 8]
                tt("dve", hT[:, :, t * 128:(t + 1) * 128], pv.rearrange("p (k n) -> p k n", k=8),
                   bc(wcol.unsqueeze(2), [128, 8, 128]), ALU.mult, pk(b) + ["pp"], [("hT", t)])

        def dense_tok(bi, ncols, evac):
            for t in range(NT):
                b = bankA()
                mm(ps[b][:, 0:ncols],
                   [(hT[:, kc, t * 128:(t + 1) * 128], wb[bi][:, kc, 0:ncols]) for kc in range(8)],
                   [("hT", t), ("wb", bi)], pk(b))
                evac(t, b)

        def dense_feat(bi, f, src, src_keys, nk=8):
            b = bankA()
            mm(ps[b][:, 0:TB],
               [(wb[bi][:, kc, f * 128:(f + 1) * 128], src[:, kc, :]) for kc in range(nk)],
               src_keys + [("wb", bi)], pk(b))
            return b

        hT_keys = [("hT", t) for t in range(NT)]
        ctr = {"u": 0}

        def conv_silu(l, b, cc, wcol0, bias_ap, dest, dest_keys):
            i = ctr["u"] % 2
            ctr["u"] += 1
            u = ubuf[i]
            a = cacc[i]
            ck = ("carry", l, cc)
            actf(u[:, 3:3 + TB], ps[b][:, 0:TB], AF.Copy, pk(b), [("ubuf", i)])
            cp("pool", u[:, 0:3], carry[:, l * 36 + cc, :], [ck, "carry"], [("ubufc", i)])
            cp("pool", carry[:, l * 36 + cc, :], u[:, TB:TB + 3], [("ubuf", i), ("ubufc", i)], [ck])
            w0 = l * PPL + wcol0
            ukeys = [("ubuf", i), ("ubufc", i), "pp"]
            actf(a[:, :], u[:, 0:TB], AF.Identity, ukeys, [("cacc", i)], scale=pp[:, w0:w0 + 1],
                 bias=(bias_ap if bias_ap is not None else 0.0))
            for k in (1, 2, 3):
                stt(a[:, :], u[:, k:k + TB], pp[:, w0 + k:w0 + k + 1], a[:, :], ALU.mult, ALU.add,
                    ukeys + [("cacc", i)], [("cacc", i)])
            actf(dest, a[:, :], AF.Silu, [("cacc", i)], dest_keys)

        def to_tok(src, src_key, dst_fn, dst_keys):
            b = bankB()
            pv = ps[b][:, :].bitcast(BF16)
            for t in range(NT):
                tr(pv[:, t * 128:(t + 1) * 128], src[:, t * 128:(t + 1) * 128], ident_b, [src_key, "cstb"],
                   pk(b, [0]))
            for t in range(NT):
                cp("dve", dst_fn(t), pv[:, t * 128:(t + 1) * 128], pk(b, [0]), [dst_keys[t]])

        def l2norm_feat(src, src_key, dest, dest_key, mult):
            actf(sqb[:, :], src, AF.Square, [src_key], ["sqb"])
            b = bankB()
            mm(ps[b][:, 0:TB], [(ones_b, sqb[:, :])], ["sqb", "cstb"], pk(b))
            actf(rq[:, :], ps[b][:, 0:TB], AF.Ln, pk(b) + ["epsT"], ["rq"], bias=epsT[:, 0:1])
            actf(rq[:, :], rq[:, :], AF.Exp, ["rq"], ["rq"], scale=-0.5)
            stt(dest, src, mult, rq[:, :], ALU.mult, ALU.mult, [src_key, "rq"], [dest_key])

        def softplus_small(dst, src_ps, bias_ap, n, keys_in, key_out):
            tt("dve", smt[:, 0:n], src_ps, bias_ap, ALU.add, keys_in, ["smt0"])
            stt(smt[:, 16:16 + n], smt[:, 0:n], -1.0, smt[:, 0:n], ALU.mult, ALU.max, ["smt0"], ["smt1"])
            actf(smt[:, 16:16 + n], smt[:, 16:16 + n], AF.Exp, ["smt1"], ["smt1"], scale=-1.0)
            actf(smt[:, 16:16 + n], smt[:, 16:16 + n], AF.Ln, ["smt1"], ["smt1"], bias=1.0)
            stt(dst, smt[:, 0:n], 0.0, smt[:, 16:16 + n], ALU.max, ALU.add, ["smt0", "smt1"], [key_out])

        def ssd_tile(l, t):
            tsl = slice(t * 128, (t + 1) * 128)
            smk = ("sm", t)
            a_ap = sm_tok[:, t, 16:32]
            b = bankB()
            mm(ps[b][:, 0:16], [(UPI, a_ap)], ["cst", smk], pk(b, [0]))
            mm(ps[b][:, 16:32], [(LOWS, a_ap)], ["cst", smk], pk(b, [0]))
            mm(ps[b][:, 32:48], [(ones_f, a_ap)], ["cst", smk], pk(b, [0]))
            actf(ea[:, :], ps[b][:, 0:48], AF.Exp, pk(b, [0]), ["ea"])
            xs3 = xs_tok[:, t, :].rearrange("p (h d) -> p h d", h=16)
            tt("dve", xdt[:, :, :], xs3, bc(sm_tok[:, t, 0:16].unsqueeze(2), [128, 16, 64]), ALU.mult,
               [("xs_tok", t), smk], ["xdt"])
            tt("pool", xdtd[:, :, :], xdt[:, :, :], bc(ea[:, 16:32].unsqueeze(2), [128, 16, 64]), ALU.mult,
               ["xdt", "ea"], ["xdtd"])
            yb = []
            fb = []
            for g in range(2):
                tt("dve", Gm[:, :, :], bc(UPI.unsqueeze(1), [128, 8, 128]),
                   bc(sm_tok[:, t, 16 + g * 8:24 + g * 8].unsqueeze(2), [128, 8, 128]), ALU.mult,
                   ["cst", smk], ["Gm"])
                sbk = [bankB(), bankB()]
                for hf in range(2):
                    mm(ps[sbk[hf]][:, :], [(LOWS, Gm[:, hf * 4:(hf + 1) * 4, :].rearrange("p h n -> p (h n)"))],
                       ["cst", "Gm"], pk(sbk[hf]))
                    actf(LT[:, hf * 4:(hf + 1) * 4, :].rearrange("p h n -> p (h n)"), ps[sbk[hf]][:, :], AF.Exp,
                         pk(sbk[hf]), [("LT", hf)])
                cbk = bankB()
                mm(ps[cbk][:, 0:128], [(BT[:, g, tsl], CT[:, g, tsl])], [("BT", g), ("CT", g)], pk(cbk, [0]))
                tt("dve", CBm[:, :], ps[cbk][:, 0:128], UPI, ALU.mult, pk(cbk, [0]) + ["cst"], ["CBm"])
                tt("dve", MT[:, :, :], LT[:, :, :], bc(CBm[:, :].unsqueeze(1), [128, 8, 128]), ALU.mult,
                   [("LT", 0), ("LT", 1), "CBm"], ["MT"])
                by = bankB()
                for hh in range(8):
                    mm(ps[by][:, hh * 64:(hh + 1) * 64], [(MT[:, hh, :], xdt[:, g * 8 + hh, :])],
                       ["MT", "xdt"], pk(by, [hh // 2]))
                bf_ = bankB()
                mm(ps[bf_][:, :], [(CT[:, g, tsl], SsB[l][:, g * 512:(g + 1) * 512])],
                   [("CT", g), ("SsB", l)], pk(bf_))
                tf3 = tokf[:, g * 512:(g + 1) * 512].rearrange("p (h d) -> p h d", h=8)
                tt("dve", tf3, ps[bf_][:, :].rearrange("p (h d) -> p h d", h=8),
                   bc(ea[:, g * 8:(g + 1) * 8].unsqueeze(2), [128, 8, 64]), ALU.mult,
                   pk(bf_) + ["ea"], [("tokf", g)])
                tt("dve", tokf[:, g * 512:(g + 1) * 512], ps[by][:, :], tokf[:, g * 512:(g + 1) * 512], ALU.add,
                   pk(by) + [("tokf", g)], [("tokf", g)])
            o = l * BPL
            tt("pool", xsD[:, :].rearrange("p (h d) -> p h d", h=16), xs3,
               bc(bp[:, o + 32:o + 48].unsqueeze(2), [128, 16, 64]), ALU.mult, [("xs_tok", t), "bp"], ["xsD"])
            TK = [("tokf", 0), ("tokf", 1)]
            tt("pool", tokf[:, :], tokf[:, :], xsD[:, :], ALU.add, TK + ["xsD"], TK)
            tt("dve", tokf[:, :], tokf[:, :], zs[:, t, :], ALU.mult, TK + [("zs", t)], TK)
            actf(junk32[:, :], tokf[:, :], AF.Square, TK, ["junk32"])
            P.add("dve", lambda e: e.tensor_reduce(out=ssq[:, 8:10], in_=junk32[:, :].rearrange("p (g d) -> p g d", g=2),
                                                   axis=AX.X, op=ALU.add), ["junk32"], ["ssq_s"])
            rstd_from(ssq[:, 8:10], 2, 1.0 / 512, "ssq_s")
            tt("dve", tokb[:, :].rearrange("p (g d) -> p g d", g=2), tokf[:, :].rearrange("p (g d) -> p g d", g=2),
               bc(rstd[:, 0:2].unsqueeze(2), [128, 2, 512]), ALU.mult, TK + ["rstd"], ["tokb"])
            b2 = bankB()
            pv = ps[b2][:, :].bitcast(BF16)
            for kc in range(8):
                tr(pv[:, kc * 128:(kc + 1) * 128], tokb[:, kc * 128:(kc + 1) * 128], ident_b, ["tokb", "cstb"],
                   pk(b2, [kc // 2]))
            w0 = l * PPL + O_SNW
            tt("dve", yT[:, :, tsl], pv.rearrange("p (k n) -> p k n", k=8),
               bc(pp[:, w0:w0 + 8].unsqueeze(2), [128, 8, 128]), ALU.mult, pk(b2) + ["pp"], [("yT", t)])
            for g in range(2):
                bs_ = bankB()
                mm(ps[bs_][:, :], [(B_tok[:, t, g * 128:(g + 1) * 128],
                                    xdtd[:, g * 8:(g + 1) * 8, :].rearrange("p h d -> p (h d)"))],
                   [("B_tok", t), "xdtd"], pk(bs_))
                s3 = Ss[l][:, g * 512:(g + 1) * 512].rearrange("p (h d) -> p h d", h=8)
                tt("pool", s3, s3, bc(ea[:, 32 + g * 8:40 + g * 8].unsqueeze(2), [128, 8, 64]), ALU.mult,
                   [("Ss", l), "ea"], [("Ss", l)])
                tt("dve", Ss[l][:, g * 512:(g + 1) * 512], ps[bs_][:, :], Ss[l][:, g * 512:(g + 1) * 512], ALU.add,
                   pk(bs_) + [("Ss", l)], [("Ss", l)])
            cp("pool", SsB[l][:, :], Ss[l][:, :], [("Ss", l)], [("SsB", l)])

        def gdn_tile(l, t):
            gst = int(os.environ.get("GST", "99"))
            tsl = slice(t * 128, (t + 1) * 128)
            smk = ("sm", t)
            g_ap = sm_tok[:, t, 32:40]
            beta_ap = sm_tok[:, t, 40:48]
            b = bankB()
            mm(ps[b][:, 0:8], [(UPI, g_ap)], ["cst", smk], pk(b, [0]))
            mm(ps[b][:, 8:16], [(LOWS, g_ap)], ["cst", smk], pk(b, [0]))
            mm(ps[b][:, 16:24], [(ones_f, g_ap)], ["cst", smk], pk(b, [0]))
            actf(eg[:, :], ps[b][:, 0:24], AF.Exp, pk(b, [0]), ["eg"])
            tt("dve", Gm[:, :, :], bc(UPI.unsqueeze(1), [128, 8, 128]), bc(g_ap.unsqueeze(2), [128, 8, 128]),
               ALU.mult, ["cst", smk], ["Gm"])
            sbk = [bankB(), bankB()]
            for hf in range(2):
                mm(ps[sbk[hf]][:, :], [(LOWS, Gm[:, hf * 4:(hf + 1) * 4, :].rearrange("p h n -> p (h n)"))],
                   ["cst", "Gm"], pk(sbk[hf]))
                actf(LT[:, hf * 4:(hf + 1) * 4, :].rearrange("p h n -> p (h n)"), ps[sbk[hf]][:, :], AF.Exp,
                     pk(sbk[hf]), [("LT", hf)])
            tt("dve", LsT[:, :, :], LT[:, :, :], bc(UPS.unsqueeze(1), [128, 8, 128]), ALU.mult,
               [("LT", 0), ("LT", 1), "cst"], ["LsT"])
            tt("dve", LT[:, :, :], LT[:, :, :], bc(UPI.unsqueeze(1), [128, 8, 128]), ALU.mult,
               [("LT", 0), ("LT", 1), "cst"], [("LT", 0), ("LT", 1)])
            k3 = k_tok[:, t, :].rearrange("p (h d) -> p h d", h=8)
            tt("dve", Rk[:, :, :], k3, bc(eg[:, 0:8].unsqueeze(2), [128, 8, 128]), ALU.mult,
               [("k_tok", t), "eg"], ["Rk"])
            tt("dve", kdec[:, :, :], k3, bc(eg[:, 8:16].unsqueeze(2), [128, 8, 128]), ALU.mult,
               [("k_tok", t), "eg"], ["kdec"])
            if gst <= 1:
                return
            for r in range(2):
                bk = bankB()
                bq = bankB()
                for hh in range(4):
                    h = r * 4 + hh
                    mm(ps[bk][:, hh * 128:(hh + 1) * 128], [(kT[:, h, tsl], kT[:, h, tsl])], [("kT", h)],
                       pk(bk, [hh]))
                    mm(ps[bq][:, hh * 128:(hh + 1) * 128], [(kT[:, h, tsl], qT[:, h, tsl])],
                       [("kT", h), ("qT", h)], pk(bq, [hh]))
                for hh in range(4):
                    h = r * 4 + hh
                    stt(XTm[:, h, :], ps[bk][:, hh * 128:(hh + 1) * 128], sm_tok[:, t, 48 + h:49 + h], LsT[:, h, :],
                        ALU.mult, ALU.mult, pk(bk, [hh]) + [smk, "LsT"], [("XT", r)])
                tt("dve", qkmT[:, r * 4:(r + 1) * 4, :].rearrange("p h n -> p (h n)"), ps[bq][:, :],
                   LT[:, r * 4:(r + 1) * 4, :].rearrange("p h n -> p (h n)"), ALU.mult, pk(bq) + [("LT", 0), ("LT", 1)],
                   [("qkmT", r)])
            if gst <= 2:
                return
            for r in range(2):
                bx = bankB()
                for hh in range(4):
                    h = r * 4 + hh
                    mm(ps[bx][:, hh * 128:(hh + 1) * 128], [(XTm[:, h, :], ident_b)], [("XT", r), "cstb"],
                       pk(bx, [hh]))
                actf(Xm[:, r * 4:(r + 1) * 4, :].rearrange("p h n -> p (h n)"), ps[bx][:, :], AF.Copy, pk(bx),
                     [("X", r)])
                tt("dve", PTm[:, r * 4:(r + 1) * 4, :], XTm[:, r * 4:(r + 1) * 4, :],
                   bc(ident_f.unsqueeze(1), [128, 4, 128]), ALU.add, [("XT", r), "cst"], [("PT", r)])
                cp("pool", PTb[:, r * 4:(r + 1) * 4, :], PTm[:, r * 4:(r + 1) * 4, :], [("PT", r)], [("PTb", r)])
            if gst <= 3:
                return
            NL = 6
            for m in range(NL):
                last = (m == NL - 1)
                bxs = []
                for r in range(2):
                    bx = bankB()
                    bxt = None if last else bankB()
                    for hh in range(4):
                        h = r * 4 + hh
                        mm(ps[bx][:, hh * 128:(hh + 1) * 128], [(XTm[:, h, :], Xm[:, h, :])],
                           [("XT", r), ("X", r)], pk(bx, [hh]))
                    if not last:
                        for hh in range(4):
                            h = r * 4 + hh
                            mm(ps[bxt][:, hh * 128:(hh + 1) * 128], [(Xm[:, h, :], XTm[:, h, :])],
                               [("XT", r), ("X", r)], pk(bxt, [hh]))
                    bxs.append((bx, bxt))
                for r in range(2):
                    bx, bxt = bxs[r]
                    actf(Xm[:, r * 4:(r + 1) * 4, :].rearrange("p h n -> p (h n)"), ps[bx][:, :], AF.Copy, pk(bx),
                         [("X", r)])
                    if not last:
                        actf(XTm[:, r * 4:(r + 1) * 4, :].rearrange("p h n -> p (h n)"), ps[bxt][:, :], AF.Copy,
                             pk(bxt), [("XT", r)])
                for r in range(2):
                    bp_ = bankB()
                    for hh in range(4):
                        h = r * 4 + hh
                        mm(ps[bp_][:, hh * 128:(hh + 1) * 128], [(Xm[:, h, :], PTb[:, h, :])],
                           [("X", r), ("PTb", r)], pk(bp_, [hh]))
                    tt("dve", PTm[:, r * 4:(r + 1) * 4, :].rearrange("p h n -> p (h n)"), ps[bp_][:, :],
                       PTm[:, r * 4:(r + 1) * 4, :].rearrange("p h n -> p (h n)"), ALU.add,
                       pk(bp_) + [("PT", r)], [("PT", r)])
                    cp("pool", PTb[:, r * 4:(r + 1) * 4, :], PTm[:, r * 4:(r + 1) * 4, :], [("PT", r)], [("PTb", r)])
            if gst <= 4:
                return
            PTBK = [("PTb", 0), ("PTb", 1)]
            for r in range(2):
                bz = bankB()
                for hh in range(4):
                    h = r * 4 + hh
                    mm(ps[bz][:, hh * 128:(hh + 1) * 128], [(Rk[:, h, :], PTb[:, h, :])], ["Rk"] + PTBK,
                       pk(bz, [hh]))
                actf(nZkT[:, r * 4:(r + 1) * 4, :].rearrange("p h n -> p (h n)"), ps[bz][:, :], AF.Copy, pk(bz),
                     [("nZkT", r)], scale=-1.0)
            for r in range(2):
                bv = bankB()
                for hh in range(4):
                    h = r * 4 + hh
                    mm(ps[bv][:, hh * 128:(hh + 1) * 128],
                       [(PTb[:, h, :], v_tok[:, t, h * 128:(h + 1) * 128]), (nZkT[:, h, :], SgB[l][:, h, :])],
                       PTBK + [("v_tok", t), ("nZkT", r), ("SgB", l)], pk(bv, [hh]))
                for hh in range(4):
                    h = r * 4 + hh
                    actf(vnew[:, h, :], ps[bv][:, hh * 128:(hh + 1) * 128], AF.Identity, pk(bv, [hh]) + [smk],
                         [("vnew", r)], scale=sm_tok[:, t, 40 + h:41 + h])
            if gst <= 5:
                return
            for r in range(2):
                bo = bankB()
                bo2 = bankB()
                for hh in range(4):
                    h = r * 4 + hh
                    mm(ps[bo][:, hh * 128:(hh + 1) * 128], [(qT[:, h, tsl], SgB[l][:, h, :])],
                       [("qT", h), ("SgB", l)], pk(bo, [hh]))
                    mm(ps[bo2][:, hh * 128:(hh + 1) * 128], [(qkmT[:, h, :], vnew[:, h, :])],
                       [("qkmT", r), ("vnew", r)], pk(bo2, [hh]))
                for hh in range(4):
                    h = r * 4 + hh
                    actf(tokf[:, h * 128:(h + 1) * 128], ps[bo][:, hh * 128:(hh + 1) * 128], AF.Identity,
                         pk(bo, [hh]) + ["eg"], [("tokf", r)], scale=eg[:, h:h + 1])
                tt("dve", tokf[:, r * 512:(r + 1) * 512], ps[bo2][:, :], tokf[:, r * 512:(r + 1) * 512], ALU.add,
                   pk(bo2) + [("tokf", r)], [("tokf", r)])
            for r in range(2):
                bs_ = bankB()
                for hh in range(4):
                    h = r * 4 + hh
                    mm(ps[bs_][:, hh * 128:(hh + 1) * 128], [(kdec[:, h, :], vnew[:, h, :])],
                       ["kdec", ("vnew", r)], pk(bs_, [hh]))
                for hh in range(4):
                    h = r * 4 + hh
                    stt(Sg[l][:, h, :], Sg[l][:, h, :], eg[:, 16 + h:17 + h], ps[bs_][:, hh * 128:(hh + 1) * 128],
                        ALU.mult, ALU.add, pk(bs_, [hh]) + [("Sg", l), "eg"], [("Sg", l)])
            cp("pool", SgB[l][:, :, :], Sg[l][:, :, :], [("Sg", l)], [("SgB", l)])
            if gst <= 6:
                return
            actf(junk32[:, :], tokf[:, :], AF.Square, [("tokf", 0), ("tokf", 1)], ["junk32"])
            P.add("dve", lambda e: e.tensor_reduce(out=ssq[:, 8:16], in_=junk32[:, :].rearrange("p (g d) -> p g d", g=8),
                                                   axis=AX.X, op=ALU.add), ["junk32"], ["ssq_s"])
            rstd_from(ssq[:, 8:16], 8, 1.0 / 128, "ssq_s")
            tt("dve", tokf[:, :].rearrange("p (g d) -> p g d", g=8), tokf[:, :].rearrange("p (g d) -> p g d", g=8),
               bc(rstd[:, 0:8].unsqueeze(2), [128, 8, 128]), ALU.mult, [("tokf", 0), ("tokf", 1), "rstd"],
               [("tokf", 0), ("tokf", 1)])
            tt("dve", tokb[:, :], tokf[:, :], zg[:, t, :], ALU.mult, [("tokf", 0), ("tokf", 1), ("zg", t)], ["tokb"])
            b2 = bankB()
            pv = ps[b2][:, :].bitcast(BF16)
            for kc in range(8):
                tr(pv[:, kc * 128:(kc + 1) * 128], tokb[:, kc * 128:(kc + 1) * 128], ident_b, ["tokb", "cstb"],
                   pk(b2, [kc // 2]))
            w0 = l * PPL + O_GNW
            actf(oT[:, :, tsl], pv.rearrange("p (k n) -> p k n", k=8), AF.Identity, pk(b2) + ["pp"], [("oT", t)],
                 scale=pp[:, w0:w0 + 1])

        def mixer(l):
            rmsnorm_T(l, O_NMIX)
            if stage <= 0:
                return
            for c in range(2):
                bi, = wuse(1, [(l, c)])
                dense_tok(bi, 512, lambda t, b, c=c: actf(zs[:, t, c * 512:(c + 1) * 512], ps[b][:, :], AF.Silu,
                                                          pk(b), [("zs", t)]))
            if stage <= 1:
                return
            for c in range(3):
                bi, = wuse(1, [(l, 2 + c)])
                for f in range(4):
                    cc = c * 4 + f
                    b = dense_feat(bi, f, hT, hT_keys)
                    bias_ap = pp[:, l * PPL + O_SCB + cc:l * PPL + O_SCB + cc + 1]
                    if cc < 8:
                        i = ctr["u"] % 2
                        conv_silu(l, b, cc, O_SCW + cc * 4, bias_ap, ftmp[i][:, :], [("ftmp", i)])
                        to_tok(ftmp[i], ("ftmp", i), lambda t, cc=cc: xs_tok[:, t, cc * 128:(cc + 1) * 128],
                               [("xs_tok", t) for t in range(NT)])
                    elif cc < 10:
                        g = cc - 8
                        conv_silu(l, b, cc, O_SCW + cc * 4, bias_ap, BT[:, g, :], [("BT", g)])
                        to_tok(BT[:, g, :], ("BT", g), lambda t, g=g: B_tok[:, t, g * 128:(g + 1) * 128],
                               [("B_tok", t) for t in range(NT)])
                    else:
                        g = cc - 10
                        conv_silu(l, b, cc, O_SCW + cc * 4, bias_ap, CT[:, g, :], [("CT", g)])
            if stage <= 2:
                return
            bi, = wuse(1, [(l, 5)])
            o = l * BPL

            def small_evac(t, b):
                kin = pk(b) + ["bp"]
                smk = ("sm", t)
                softplus_small(sm_tok[:, t, 0:16], ps[b][:, 0:16], bp[:, o:o + 16], 16, kin, smk)
                tt("dve", sm_tok[:, t, 16:32], sm_tok[:, t, 0:16], nexpA[:, l * 24:l * 24 + 16], ALU.mult,
                   [smk, "nexpA"], [smk])
                softplus_small(sm_tok[:, t, 32:40], ps[b][:, 16:24], bp[:, o + 56:o + 64], 8, kin + [smk], smk)
                tt("dve", sm_tok[:, t, 32:40], sm_tok[:, t, 32:40], nexpA[:, l * 24 + 16:l * 24 + 24], ALU.mult,
                   [smk, "nexpA"], [smk])
                actf(sm_tok[:, t, 40:48], ps[b][:, 24:32], AF.Sigmoid, pk(b) + [smk], [smk])
                ts("dve", sm_tok[:, t, 48:56], sm_tok[:, t, 40:48], -1.0, ALU.mult, [smk], [smk])
            dense_tok(bi, 32, small_evac)
            if stage <= 3:
                return
            for t in range(NT):
                ssd_tile(l, t)
            if stage <= 4:
                return
            for c in range(6):
                bi, = wuse(1, [(l, 6 + c)])
                for f in range(4):
                    cc = c * 4 + f
                    h = cc % 8
                    b = dense_feat(bi, f, hT, hT_keys)
                    i = ctr["u"] % 2
                    conv_silu(l, b, 12 + cc, O_GCW + cc * 4, None, ftmp[i][:, :], [("ftmp", i)])
                    if cc < 8:
                        l2norm_feat(ftmp[i][:, :], ("ftmp", i), qT[:, h, :], ("qT", h), 128.0 ** -0.5)
                    elif cc < 16:
                        l2norm_feat(ftmp[i][:, :], ("ftmp", i), kT[:, h, :], ("kT", h), 1.0)
                        to_tok(kT[:, h, :], ("kT", h), lambda t, h=h: k_tok[:, t, h * 128:(h + 1) * 128],
                               [("k_tok", t) for t in range(NT)])
                    else:
                        to_tok(ftmp[i], ("ftmp", i), lambda t, h=h: v_tok[:, t, h * 128:(h + 1) * 128],
                               [("v_tok", t) for t in range(NT)])
            for c in range(2):
                bi, = wuse(1, [(l, 12 + c)])
                dense_tok(bi, 512, lambda t, b, c=c: actf(zg[:, t, c * 512:(c + 1) * 512], ps[b][:, :], AF.Silu,
                                                          pk(b), [("zg", t)]))
            if stage <= 5:
                return
            for t in range(NT):
                gdn_tile(l, t)
            if stage <= 6:
                return
            for c in range(4):
                bi, = wuse(1, [(l, 14 + c)])
                for f in range(4):
                    j = c * 4 + f
                    b = dense_feat(bi, f, hT, hT_keys)
                    actf(gT[:, j, :], ps[b][:, 0:TB], AF.Sigmoid, pk(b), [("gT", j)])
            yT_keys = [("yT", t) for t in range(NT)]
            oT_keys = [("oT", t) for t in range(NT)]
            for c in range(2):
                b1, b2 = wuse(2, [(l, 18 + c), (l, 20 + c)])
                for f in range(4):
                    j = c * 4 + f
                    pa = dense_feat(b1, f, yT, yT_keys)
                    pb_ = dense_feat(b2, f, oT, oT_keys)
                    i = ctr["u"] % 2
                    ctr["u"] += 1
                    tt("dve", mtmp[i][:, 0, :], ps[pa][:, 0:TB], gT[:, j, :], ALU.mult, pk(pa) + [("gT", j)],
                       [("mtmp", i, 0)])
                    tt("dve", mtmp[i][:, 1, :], ps[pb_][:, 0:TB], gT[:, 8 + j, :], ALU.mult,
                       pk(pb_) + [("gT", 8 + j)], [("mtmp", i, 1)])
                    tt("pool", hT[:, j, :], mtmp[i][:, 0, :], mtmp[i][:, 1, :], ALU.add,
                       [("mtmp", i, 0), ("mtmp", i, 1)], hT_keys)
            if stage <= 7:
                return
            for c in range(2):
                bi, = wuse(1, [(l, 22 + c)])
                dense_tok(bi, 512, lambda t, b, c=c: tt("dve", xb[:, t, c * 512:(c + 1) * 512], ps[b][:, :],
                                                        xb[:, t, c * 512:(c + 1) * 512], ALU.add,
                                                        pk(b) + [("xb", t)], [("xb", t)]))

        def ffn(l):
            rmsnorm_T(l, O_NFFN)
            for q in range(6):
                nf = 4 if q < 5 else 2
                bg, bu = wuse(2, [(l, 24 + 2 * q), (l, 25 + 2 * q)])
                for f in range(nf):
                    j = q * 4 + f
                    pg = dense_feat(bg, f, hT, hT_keys)
                    pu = dense_feat(bu, f, hT, hT_keys)
                    i = ctr["u"] % 2
                    ctr["u"] += 1
                    actf(cacc[i][:, :], ps[pg][:, 0:TB], AF.Silu, pk(pg), [("cacc", i)])
                    tt("dve", actT[:, j, :], ps[pu][:, 0:TB], cacc[i][:, :], ALU.mult, pk(pu) + [("cacc", i)],
                       [("actT", j)])
            for c in range(2):
                banks = [bankA() for _ in range(NT)]
                for kg in range(3):
                    bi, = wuse(1, [(l, 36 + c * 3 + kg)])
                    nk = 8 if kg < 2 else 6
                    for t in range(NT):
                        def fn(e, t=t, kg=kg, nk=nk, bi=bi, banks=banks):
                            ins = None
                            for kk in range(nk):
                                kc = kg * 8 + kk
                                ins = e.matmul(ps[banks[t]][:, :], actT[:, kc, t * 128:(t + 1) * 128],
                                               wb[bi][:, kk, :], start=(kc == 0), stop=(kc == 21))
                            return ins
                        P.add("pe", fn, [("actT", kg * 8 + kk) for kk in range(nk)] + [("wb", bi)], pk(banks[t]))
                for t in range(NT):
                    tt("dve", xb[:, t, c * 512:(c + 1) * 512], ps[banks[t]][:, :], xb[:, t, c * 512:(c + 1) * 512],
                       ALU.add, pk(banks[t]) + [("xb", t)], [("xb", t)])

        def dump(nm, src_fn):
            if nm in dbg_d:
                for t in range(NT):
                    P.dma(dbg_d[nm][t * 128:(t + 1) * 128, :], src_fn(t), [("xb", t)], [("dbg", nm, t)], "dbg%d" % t)

        fin_ops = []
        for blk in range(NB):
            r0 = blk * TB
            P.dma(xb[:, :, :], x_in[r0:r0 + TB, :].rearrange("(t p) d -> p t d", p=128), [],
                  [("xb", t) for t in range(NT)], "xb")
            for l in range(depth):
                mixer(l)
                if dbgsb and blk == 0 and l == 0:
                    TT = range(NT)
                    for nm_, ten, kk_ in (
                            ("sm_tok", sm_tok, [("sm", t) for t in TT]), ("zs", zs, [("zs", t) for t in TT]),
                            ("xs_tok", xs_tok, [("xs_tok", t) for t in TT]), ("BT", BT, [("BT", g) for g in range(2)]),
                            ("CT", CT, [("CT", g) for g in range(2)]), ("B_tok", B_tok, [("B_tok", t) for t in TT]),
                            ("yT", yT, [("yT", t) for t in TT]), ("qT", qT, [("qT", h) for h in range(8)]),
                            ("kT", kT, [("kT", h) for h in range(8)]), ("k_tok", k_tok, [("k_tok", t) for t in TT]),
                            ("v_tok", v_tok, [("v_tok", t) for t in TT]), ("zg", zg, [("zg", t) for t in TT]),
                            ("oT", oT, [("oT", t) for t in TT]), ("gT", gT, [("gT", j) for j in range(16)]),
                            ("hT", hT, [("hT", t) for t in TT]), ("Ss0", Ss[0], [("Ss", 0)]), ("Sg0", Sg[0], [("Sg", 0)]),
                            ("ea", ea, ["ea"]), ("eg", eg, ["eg"]), ("PTm", PTm, [("PT", 0), ("PT", 1)]),
                            ("Xm", Xm, [("X", 0), ("X", 1)]), ("tokf", tokf, [("tokf", 0), ("tokf", 1)]),
                            ("vnew", vnew, [("vnew", 0), ("vnew", 1)]), ("LT", LT, [("LT", 0), ("LT", 1)])):
                        shp = list(ten.shape)
                        fr = int(np.prod(shp[1:]))
                        dd = nc.dram_tensor("sb_" + nm_, [128, fr], ten.dtype, kind="ExternalOutput").ap()
                        src = ten[:, :] if len(shp) == 2 else ten[:, :, :].rearrange("p a b -> p (a b)")
                        P.dma(dd, src, kk_, [("sbdump", nm_)], "d_" + nm_)
                if blk == 0:
                    dump("mix%d" % l, lambda t: xb[:, t, :])
                if stage >= 100:
                    ffn(l)
                if blk == 0:
                    dump("ffn%d" % l, lambda t: xb[:, t, :])
            for t in range(NT):
                actf(junk[:, :], xb[:, t, :], AF.Square, [("xb", t)], ["junk", ("ssq", t)], accum=ssq[:, t:t + 1])
            actf(lnv[:, 0:NT], ssq[:, 0:NT], AF.Ln, [("ssq", t) for t in range(NT)] + ["epsT"], ["lnv"],
                 scale=1.0 / D, bias=epsT[:, 0:1])
            actf(rstd[:, 0:NT], lnv[:, 0:NT], AF.Exp, ["lnv"], ["rstd"], scale=-0.5)
            for t in range(NT):
                stt(xb[:, t, :], xb[:, t, :], rstd[:, t:t + 1], bp[:, depth * BPL:depth * BPL + D], ALU.mult, ALU.mult,
                    [("xb", t), "rstd", "bp"], [("xb", t)])
            fin_ops.append(P.dma(out_d[r0:r0 + TB, :].rearrange("(t p) d -> p t d", p=128), xb[:, :, :],
                                 [("xb", t) for t in range(NT)], [("out", blk)], "out"))
        P.emit([fin_ops[-1]])
    return nc


def host_params(inp, depth=DEPTH):
    f = np.float32
    pp = np.zeros((128, depth * PPL), f)
    bp = np.zeros((128, depth * BPL + D), f)

    def fm(v):
        return np.ascontiguousarray(np.asarray(v, f).reshape(-1, 128).T)
    for l in range(depth):
        o = l * PPL
        pp[:, o + O_NMIX:o + O_NMIX + 8] = fm(inp["norm_mix_w"][l])
        pp[:, o + O_NFFN:o + O_NFFN + 8] = fm(inp["norm_ffn_w"][l])
        pp[:, o + O_SNW:o + O_SNW + 8] = fm(inp["ssm_norm_w"][l])
        pp[:, o + O_GNW:o + O_GNW + 1] = fm(inp["gdn_norm_w"][l])
        scw = np.asarray(inp["ssm_conv_w"][l], f)
        pp[:, o + O_SCW:o + O_SCW + 48] = scw.reshape(4, 12, 128).transpose(2, 1, 0).reshape(128, 48)
        pp[:, o + O_SCB:o + O_SCB + 12] = fm(inp["ssm_conv_b"][l])
        gcw = np.asarray(inp["gdn_conv_w"][l], f)
        pp[:, o + O_GCW:o + O_GCW + 96] = gcw.reshape(4, 24, 128).transpose(2, 1, 0).reshape(128, 96)
        b = l * BPL
        bp[:, b:b + 16] = np.asarray(inp["ssm_dt_bias"][l], f)[None, :]
        bp[:, b + 16:b + 32] = np.asarray(inp["ssm_a_log"][l], f)[None, :]
        bp[:, b + 32:b + 48] = np.asarray(inp["ssm_d"][l], f)[None, :]
        bp[:, b + 48:b + 56] = np.asarray(inp["gdn_a_log"][l], f)[None, :]
        bp[:, b + 56:b + 64] = np.asarray(inp["gdn_dt_bias"][l], f)[None, :]
    bp[:, depth * BPL:] = np.asarray(inp["final_norm_w"], f)[None, :]
    r = np.arange(128)
    cst = np.concatenate([np.eye(128), (r[:, None] > r[None, :]), (r[:, None] <= r[None, :]),
                          (r[None, :] > r[:, None]), np.ones((128, 128))], axis=1).astype(f)
    return pp, bp, cst


_NC_CACHE = {}


def kernel(**inputs):
    inp = {k: np.asarray(v) for k, v in inputs.items()}
    x = np.ascontiguousarray(inp["x"], dtype=np.float32)
    pp, bp, cst = host_params(inp)
    if "nc" not in _NC_CACHE:
        _NC_CACHE["nc"] = build()
    nc = _NC_CACHE["nc"]
    shared = {k: np.ascontiguousarray(inp[k], dtype=np.float32) for k in
              ("w_in", "w_proj_ssm", "w_proj_gdn", "w_out", "w_ffn_in", "w_ffn_down")}
    in_maps = []
    for c in range(BATCH):
        m = dict(shared)
        m["x"] = x[c]
        m["pp"] = pp
        m["bp"] = bp
        m["cst"] = cst
        in_maps.append(m)
    res = run_bass_kernel_spmd(nc, in_maps, core_ids=list(range(BATCH)))
    return np.stack([np.asarray(r["out"], dtype=np.float32) for r in res.results], axis=0)
```

```python
import contextlib
import os
import numpy as np
import concourse.bass as bass
import concourse.mybir as mybir
from concourse.bass_utils import run_bass_kernel_spmd

F32 = mybir.dt.float32
F32R = mybir.dt.float32
BF16 = mybir.dt.bfloat16
AF = mybir.ActivationFunctionType
ALU = mybir.AluOpType
AX = mybir.AxisListType

D = 1024
SEQ = 4096
BATCH = 8
DEPTH = 2
IN_DIM = 8736
FFN_H = 2816
TB = 256
NT = TB // 128
EPS = 1e-6
PPL = 181
O_NMIX, O_NFFN, O_SNW, O_GNW, O_SCW, O_SCB, O_GCW = 0, 8, 16, 24, 25, 73, 85
BPL = 64
NCH = 42


class _Op:
    __slots__ = ("eng", "fn", "reads", "writes", "dsem", "gfinal", "deps", "signal", "tick", "idx")


class Prog:
    ENGINES = ("pe", "act", "dve", "pool", "sp")

    def __init__(self, nc):
        self.nc = nc
        self.ops = []

    def add(self, eng, fn, reads=(), writes=(), dsem=None, gfinal=False):
        op = _Op()
        op.eng, op.fn, op.reads, op.writes, op.dsem, op.gfinal = eng, fn, tuple(reads), tuple(writes), dsem, gfinal
        op.deps, op.signal, op.tick = [], False, None
        op.idx = len(self.ops)
        self.ops.append(op)
        return op

    def dma(self, out, in_, reads, writes, dsem, eng="sp", gfinal=False):
        return self.add(eng, lambda e: e.dma_start(out=out, in_=in_), reads, writes, dsem, gfinal)

    def emit(self, final_wait_ops=()):
        nc = self.nc
        ops = self.ops
        last_w = {}
        readers = {}
        for op in ops:
            deps = {}
            for t in op.reads:
                w = last_w.get(t)
                if w is not None:
                    deps[w.idx] = w
            for t in op.writes:
                w = last_w.get(t)
                if w is not None and not (op.eng == "pe" and w.eng == "pe" and isinstance(t, tuple)
                                          and t[0] == "ps"):
                    deps[w.idx] = w
                for r in readers.get(t, ()):
                    deps[r.idx] = r
            deps.pop(op.idx, None)
            op.deps = list(deps.values())
            for t in op.reads:
                readers.setdefault(t, []).append(op)
            for t in op.writes:
                last_w[t] = op
                readers[t] = []
        for op in ops:
            for d in op.deps:
                d.signal = True
        for op in final_wait_ops:
            op.signal = True
        counts = {}
        for op in ops:
            if op.dsem is not None:
                k = ("dma", op.dsem)
                counts[k] = counts.get(k, 0) + 16
                op.tick = (k, counts[k])
            elif op.signal:
                k = ("eng", op.eng)
                c = counts.get(k, 0) + 1
                counts[k] = c
                op.tick = (k + ((c - 1) // 20000,), (c - 1) % 20000 + 1)
        for op in ops:
            if op.dsem is not None and op.gfinal:
                op.tick = (op.tick[0], counts[op.tick[0]])
        keys = []
        for op in ops:
            if op.tick is not None and op.tick[0] not in keys:
                keys.append(op.tick[0])
        with contextlib.ExitStack() as st:
            sems = {}
            for i, k in enumerate(keys):
                sems[k] = st.enter_context(nc.semaphore("sem%d" % i))
            block = st.enter_context(nc.Block())
            per_eng = {e: [o for o in ops if o.eng == e] for e in self.ENGINES}
            final = list(final_wait_ops)

            def run_stream(eng_name, e):
                known = {}
                for op in per_eng[eng_name]:
                    need = {}
                    for d in op.deps:
                        k, v = d.tick
                        if known.get(k, -1) >= v:
                            continue
                        if need.get(k, -1) < v:
                            need[k] = v
                    for k, v in need.items():
                        e.wait_ge(sems[k], v)
                        known[k] = v
                    ins = op.fn(e)
                    if op.tick is not None:
                        k, v = op.tick
                        ins.then_inc(sems[k], 16 if k[0] == "dma" else 1)
                if eng_name == "sp":
                    for op in final:
                        k, v = op.tick
                        if known.get(k, -1) < v:
                            e.wait_ge(sems[k], v)
                            known[k] = v

            @block.sync
            def _(e):
                run_stream("sp", e)

            @block.tensor
            def _(e):
                run_stream("pe", e)

            @block.scalar
            def _(e):
                run_stream("act", e)

            @block.vector
            def _(e):
                run_stream("dve", e)

            @block.gpsimd
            def _(e):
                run_stream("pool", e)


def chunk_specs():
    sp = []
    for c in range(2):
        sp.append(("w_in", 0, 1024, c * 512, 512))
    for c in range(3):
        sp.append(("w_in", 0, 1024, 1024 + c * 512, 512))
    sp.append(("small", 0, 1024, 0, 32))
    for c in range(6):
        sp.append(("w_in", 0, 1024, 2576 + c * 512, 512))
    for c in range(2):
        sp.append(("w_in", 0, 1024, 5648 + c * 512, 512))
    for c in range(4):
        sp.append(("w_in", 0, 1024, 6688 + c * 512, 512))
    for nm in ("w_proj_ssm", "w_proj_gdn", "w_out"):
        for c in range(2):
            sp.append((nm, 0, 1024, c * 512, 512))
    for q in range(6):
        w = min(512, FFN_H - q * 512)
        sp.append(("w_ffn_in", 0, 1024, q * 512, w))
        sp.append(("w_ffn_in", 0, 1024, FFN_H + q * 512, w))
    for c in range(2):
        for kg in range(3):
            r0 = kg * 1024
            sp.append(("w_ffn_down", r0, min(1024, FFN_H - r0), c * 512, 512))
    assert len(sp) == NCH
    return sp


USE_ORDER = ([0, 1, 2, 3, 4, 5] + list(range(6, 14)) + [14, 15, 16, 17, 18, 20, 19, 21, 22, 23]
             + list(range(24, 42)))


def build(NB=SEQ // TB, depth=DEPTH, dbg=(), stage=100, dbgsb=False):
    nc = bass.Bass("TRN2", target_bir_lowering=False)
    S = NB * TB
    x_in = nc.dram_tensor("x", [S, D], F32, kind="ExternalInput").ap()
    wd = {}
    wd["w_in"] = nc.dram_tensor("w_in", [depth, D, IN_DIM], F32, kind="ExternalInput").ap()
    wd["w_proj_ssm"] = nc.dram_tensor("w_proj_ssm", [depth, D, D], F32, kind="ExternalInput").ap()
    wd["w_proj_gdn"] = nc.dram_tensor("w_proj_gdn", [depth, D, D], F32, kind="ExternalInput").ap()
    wd["w_out"] = nc.dram_tensor("w_out", [depth, D, D], F32, kind="ExternalInput").ap()
    wd["w_ffn_in"] = nc.dram_tensor("w_ffn_in", [depth, D, 2 * FFN_H], F32, kind="ExternalInput").ap()
    wd["w_ffn_down"] = nc.dram_tensor("w_ffn_down", [depth, FFN_H, D], F32, kind="ExternalInput").ap()
    pp_d = nc.dram_tensor("pp", [128, depth * PPL], F32, kind="ExternalInput").ap()
    bp_d = nc.dram_tensor("bp", [128, depth * BPL + D], F32, kind="ExternalInput").ap()
    cst_d = nc.dram_tensor("cst", [128, 5 * 128], F32, kind="ExternalInput").ap()
    out_d = nc.dram_tensor("out", [S, D], F32, kind="ExternalOutput").ap()
    wscr = nc.dram_tensor("wscr", [depth * NCH, 128, 8 * 512], BF16, kind="Internal").ap()
    dbg_d = {}
    for nm, shp in dbg:
        dbg_d[nm] = nc.dram_tensor("dbg_" + nm, list(shp), F32, kind="ExternalOutput").ap()

    specs = chunk_specs()
    P = Prog(nc)
    with contextlib.ExitStack() as st:
        def sb(name, shape, dt):
            return st.enter_context(nc.sbuf_tensor("s_" + name, list(shape), dt))

        ps = [st.enter_context(nc.psum_tensor("ps%d" % i, [128, 512], F32)) for i in range(8)]

        def pk(b, qs=None):
            return [("ps", b)]

        rrA = [0]
        rrB = [0]

        def bankA():
            b = rrA[0] % 4
            rrA[0] += 1
            return b

        def bankB():
            b = 4 + rrB[0] % 4
            rrB[0] += 1
            return b

        cst = sb("cst", [128, 5 * 128], F32)
        ident_f = cst[:, 0:128]
        LOWS = cst[:, 128:256]
        UPI = cst[:, 256:384]
        UPS = cst[:, 384:512]
        ones_f = cst[:, 512:640]
        cstb = sb("cstb", [128, 2 * 128], BF16)
        ident_b = cstb[:, 0:128]
        ones_b = cstb[:, 128:256]
        pp = sb("pp", [128, depth * PPL], F32)
        bp = sb("bp", [128, depth * BPL + D], F32)
        nexpA = sb("nexpA", [128, depth * 24], F32)
        epsT = sb("epsT", [128, 1], F32)
        xb = sb("xb", [128, NT, D], F32)
        hT = sb("hT", [128, 8, TB], BF16)
        wb = [sb("wb%d" % i, [128, 8, 512], BF16) for i in range(3)]
        zs = sb("zs", [128, NT, D], BF16)
        zg = sb("zg", [128, NT, D], BF16)
        gT = sb("gT", [128, 16, TB], BF16)
        xs_tok = sb("xs_tok", [128, NT, D], BF16)
        B_tok = sb("B_tok", [128, NT, 256], BF16)
        BT = sb("BT", [128, 2, TB], BF16)
        CT = sb("CT", [128, 2, TB], BF16)
        qT = sb("qT", [128, 8, TB], BF16)
        kT = sb("kT", [128, 8, TB], BF16)
        k_tok = sb("k_tok", [128, NT, D], BF16)
        v_tok = sb("v_tok", [128, NT, D], BF16)
        yT = sb("yT", [128, 8, TB], BF16)
        oT = sb("oT", [128, 8, TB], BF16)
        actT = sb("actT", [128, 22, TB], BF16)
        sm_tok = sb("sm_tok", [128, NT, 64], F32)
        carry = sb("carry", [128, depth * 36, 3], F32)
        ubuf = [sb("ubuf%d" % i, [128, TB + 3], F32) for i in range(2)]
        cacc = [sb("cacc%d" % i, [128, TB], F32) for i in range(2)]
        ftmp = [sb("ftmp%d" % i, [128, TB], BF16) for i in range(2)]
        ftq = [sb("ftq%d" % i, [128, TB], BF16) for i in range(4)]
        sqb = sb("sqb", [128, TB], BF16)
        rq = sb("rq", [128, TB], F32)
        junk = sb("junk", [128, D], BF16)
        junk32 = sb("junk32", [128, D], F32)
        ssq = sb("ssq", [128, 16], F32)
        lnv = sb("lnv", [128, 16], F32)
        rstd = sb("rstd", [128, 16], F32)
        xn = sb("xn", [128, D], BF16)
        Gm = sb("Gm", [128, 8, 128], F32)
        LT = sb("LT", [128, 8, 128], F32)
        LsT = sb("LsT", [128, 8, 128], F32)
        ea = sb("ea", [128, 48], F32)
        eg = sb("eg", [128, 24], F32)
        CBm = sb("CBm", [128, 128], F32)
        MT = sb("MT", [128, 8, 128], BF16)
        xdt = sb("xdt", [128, 16, 64], BF16)
        xdtd = sb("xdtd", [128, 16, 64], BF16)
        tokf = sb("tokf", [128, D], F32)
        xsD = sb("xsD", [128, D], F32)
        tokb = sb("tokb", [128, D], BF16)
        smt = sb("smt", [128, 64], F32)
        mtmp = [sb("mtmp%d" % i, [128, 2, TB], F32) for i in range(2)]
        Xm = sb("Xm", [128, 8, 128], BF16)
        XTm = sb("XTm", [128, 8, 128], BF16)
        PTm = sb("PTm", [128, 8, 128], F32)
        PTb = sb("PTb", [128, 8, 128], BF16)
        qkmT = sb("qkmT", [128, 8, 128], BF16)
        Rk = sb("Rk", [128, 8, 128], BF16)
        kdec = sb("kdec", [128, 8, 128], BF16)
        nZkT = sb("nZkT", [128, 8, 128], BF16)
        vnew = sb("vnew", [128, 8, 128], BF16)
        Ss = [sb("Ss%d" % l, [128, D], F32) for l in range(depth)]
        SsB = [sb("SsB%d" % l, [128, D], BF16) for l in range(depth)]
        Sg = [sb("Sg%d" % l, [128, 8, 128], F32) for l in range(depth)]
        SgB = [sb("SgB%d" % l, [128, 8, 128], BF16) for l in range(depth)]

        def mm(out_ap, pairs, reads, writes):
            def fn(e):
                n = len(pairs)
                ins = None
                for i, (l_, r_) in enumerate(pairs):
                    ins = e.matmul(out_ap, l_, r_, start=(i == 0), stop=(i == n - 1))
                return ins
            return P.add("pe", fn, reads, writes)

        def tr(out_ap, in_ap, idn, reads, writes):
            return P.add("pe", lambda e: e.transpose(out_ap, in_ap, idn), reads, writes)

        def actf(out, in_, func, reads, writes, scale=1.0, bias=0.0, accum=None):
            def fn(e):
                kw = {}
                if accum is not None:
                    kw["accum_out"] = accum
                return e.activation(out=out, in_=in_, func=func, bias=bias, scale=scale, **kw)
            return P.add("act", fn, reads, writes)

        def tt(eng, out, in0, in1, op, reads, writes):
            return P.add(eng, lambda e: e.tensor_tensor(out=out, in0=in0, in1=in1, op=op), reads, writes)

        def ts(eng, out, in0, s1, op0, reads, writes, s2=None, op1=None):
            if op1 is None:
                return P.add(eng, lambda e: e.tensor_scalar(out=out, in0=in0, scalar1=s1, scalar2=None, op0=op0),
                             reads, writes)
            return P.add(eng, lambda e: e.tensor_scalar(out=out, in0=in0, scalar1=s1, scalar2=s2, op0=op0, op1=op1),
                         reads, writes)

        def stt(out, in0, scalar, in1, op0, op1, reads, writes):
            return P.add("dve", lambda e: e.scalar_tensor_tensor(out=out, in0=in0, scalar=scalar, in1=in1,
                                                                  op0=op0, op1=op1), reads, writes)

        def cp(eng, out, in_, reads, writes):
            return P.add(eng, lambda e: e.tensor_copy(out=out, in_=in_), reads, writes)

        def rstd_from(ssum_ap, n, inv_count, key):
            actf(lnv[:, 0:n], ssum_ap, AF.Ln, [key, "epsT"], ["lnv"], scale=inv_count, bias=epsT[:, 0:1])
            actf(rstd[:, 0:n], lnv[:, 0:n], AF.Exp, ["lnv"], ["rstd"], scale=-0.5)

        def bc(ap, shape):
            return ap.to_broadcast(list(shape))

        P.dma(cst[:, :], cst_d, [], ["cst"], "par", gfinal=True)
        P.dma(pp[:, :], pp_d, [], ["pp"], "par", gfinal=True)
        P.dma(bp[:, :], bp_d, [], ["bp"], "par", gfinal=True)
        cp("dve", ident_b, ident_f, ["cst"], ["cstb"])
        cp("dve", ones_b, ones_f, ["cst"], ["cstb"])
        P.add("dve", lambda e: e.memset(epsT[:, :], EPS), [], ["epsT"])
        P.add("dve", lambda e: e.memset(carry[:, :, :], 0.0), [], ["carry"])
        P.add("dve", lambda e: e.memset(sm_tok[:, :, :], 0.0), [], [("sm", t) for t in range(NT)])
        for l in range(depth):
            P.add("dve", lambda e, l=l: e.memset(Ss[l][:, :], 0.0), [], [("Ss", l)])
            P.add("dve", lambda e, l=l: e.memset(SsB[l][:, :], 0.0), [], [("SsB", l)])
            P.add("dve", lambda e, l=l: e.memset(Sg[l][:, :, :], 0.0), [], [("Sg", l)])
            P.add("dve", lambda e, l=l: e.memset(SgB[l][:, :, :], 0.0), [], [("SgB", l)])
            o = l * BPL
            actf(nexpA[:, l * 24:l * 24 + 16], bp[:, o + 16:o + 32], AF.Exp, ["bp"], ["nexpA_t"])
            actf(nexpA[:, l * 24 + 16:l * 24 + 24], bp[:, o + 48:o + 56], AF.Exp, ["bp"], ["nexpA_t"])
        ts("dve", nexpA[:, :], nexpA[:, :], -1.0, ALU.mult, ["nexpA_t"], ["nexpA"])

        ncv = [0]
        for l in range(depth):
            for cid in USE_ORDER:
                nm, r0, nr, c0, ncol = specs[cid]
                dst = wscr[l * NCH + cid].rearrange("p (k n) -> p k n", k=8)
                key = ("wscr", l, cid)
                slot = ncv[0] % 8
                ncv[0] += 1
                ds = "cv%d" % slot
                sk = ("cvslot", slot)
                if nm == "small":
                    for (cs, dcol, w) in ((2560, 0, 16), (6672, 16, 16)):
                        P.dma(dst[:, :, dcol:dcol + w],
                              wd["w_in"][l, :, cs:cs + w].rearrange("(k p) n -> p k n", p=128),
                              [], [key, sk], ds, eng="pool")
                else:
                    kc = nr // 128
                    P.dma(dst[:, 0:kc, 0:ncol],
                          wd[nm][l, r0:r0 + nr, c0:c0 + ncol].rearrange("(k p) n -> p k n", p=128),
                          [], [key, sk], ds, eng="pool")

        order = []
        n_use = {0: 0, 1: 2, 2: 5, 3: 6, 4: 6, 5: 14, 6: 14, 7: 22}.get(stage, 24 if stage < 100 else 42)
        for blk in range(NB):
            for l in range(depth):
                for cid in USE_ORDER[:n_use]:
                    order.append((l, cid))
        wstate = {"pos": 0, "issued": 0}

        def wuse(k=1, expect=None):
            n = wstate["pos"]
            lim = min(n + 3, len(order))
            while wstate["issued"] < lim:
                i = wstate["issued"]
                l_, cid = order[i]
                _nm, _r0, _nr, _c0, _ncol = specs[cid]
                _kc = _nr // 128
                P.dma(wb[i % 3][:, 0:_kc, 0:_ncol],
                      wscr[l_ * NCH + cid].rearrange("p (k n) -> p k n", k=8)[:, 0:_kc, 0:_ncol],
                      [("wscr", l_, cid)], [("wb", i % 3)], "wb%d" % (i % 3))
                wstate["issued"] += 1
            res = []
            for j in range(k):
                if expect is not None:
                    assert order[n + j] == expect[j], (order[n + j], expect[j])
                res.append(((n + j) % 3))
            wstate["pos"] += k
            return res

        def rmsnorm_T(l, off):
            for t in range(NT):
                actf(junk[:, :], xb[:, t, :], AF.Square, [("xb", t)], ["junk", ("ssq", t)], accum=ssq[:, t:t + 1])
            actf(lnv[:, 0:NT], ssq[:, 0:NT], AF.Ln, [("ssq", t) for t in range(NT)] + ["epsT"], ["lnv"],
                 scale=1.0 / D, bias=epsT[:, 0:1])
            actf(rstd[:, 0:NT], lnv[:, 0:NT], AF.Exp, ["lnv"], ["rstd"], scale=-0.5)
            for t in range(NT):
                ts("dve", xn[:, :], xb[:, t, :], rstd[:, t:t + 1], ALU.mult, [("xb", t), "rstd"], ["xn"])
                b = bankB()
                pv = ps[b][:, :].bitcast(BF16)
                for kc in range(8):
                    tr(pv[:, kc * 128:(kc + 1) * 128], xn[:, kc * 128:(kc + 1) * 128], ident_b,
                       ["xn", "cstb"], pk(b, [kc // 2]))
                wcol = pp[:, l * PPL + off:l * PPL + off + 8]
                tt("dve", hT[:, :, t * 128:(t + 1) * 128], pv.rearrange("p (k n) -> p k n", k=8),
                   bc(wcol.unsqueeze(2), [128, 8, 128]), ALU.mult, pk(b) + ["pp"], [("hT", t)])

        def dense_tok(bi, ncols, evac):
            for t in range(NT):
                b = bankA()
                mm(ps[b][:, 0:ncols],
                   [(hT[:, kc, t * 128:(t + 1) * 128], wb[bi][:, kc, 0:ncols]) for kc in range(8)],
                   [("hT", t), ("wb", bi)], pk(b))
                evac(t, b)

        def dense_feat(bi, f, src, src_keys, nk=8):
            b = bankA()
            mm(ps[b][:, 0:TB],
               [(wb[bi][:, kc, f * 128:(f + 1) * 128], src[:, kc, :]) for kc in range(nk)],
               src_keys + [("wb", bi)], pk(b))
            return b

        hT_keys = [("hT", t) for t in range(NT)]
        ctr = {"u": 0}

        def conv_silu(l, b, cc, wcol0, bias_ap, dest, dest_keys):
            i = ctr["u"] % 2
            ctr["u"] += 1
            u = ubuf[i]
            a = cacc[i]
            ck = ("carry", l, cc)
            actf(u[:, 3:3 + TB], ps[b][:, 0:TB], AF.Copy, pk(b), [("ubuf", i)])
            cp("pool", u[:, 0:3], carry[:, l * 36 + cc, :], [ck, "carry"], [("ubufc", i)])
            cp("pool", carry[:, l * 36 + cc, :], u[:, TB:TB + 3], [("ubuf", i), ("ubufc", i)], [ck])
            w0 = l * PPL + wcol0
            ukeys = [("ubuf", i), ("ubufc", i), "pp"]
            actf(a[:, :], u[:, 0:TB], AF.Identity, ukeys, [("cacc", i)], scale=pp[:, w0:w0 + 1],
                 bias=(bias_ap if bias_ap is not None else 0.0))
            for k in (1, 2, 3):
                stt(a[:, :], u[:, k:k + TB], pp[:, w0 + k:w0 + k + 1], a[:, :], ALU.mult, ALU.add,
                    ukeys + [("cacc", i)], [("cacc", i)])
            actf(dest, a[:, :], AF.Silu, [("cacc", i)], dest_keys)

        def to_tok(src, src_key, dst_fn, dst_keys):
            b = bankB()
            pv = ps[b][:, :].bitcast(BF16)
            for t in range(NT):
                tr(pv[:, t * 128:(t + 1) * 128], src[:, t * 128:(t + 1) * 128], ident_b, [src_key, "cstb"],
                   pk(b, [0]))
            for t in range(NT):
                cp("dve", dst_fn(t), pv[:, t * 128:(t + 1) * 128], pk(b, [0]), [dst_keys[t]])

        def l2norm_feat(src, src_key, dest, dest_key, mult):
            actf(sqb[:, :], src, AF.Square, [src_key], ["sqb"])
            b = bankB()
            mm(ps[b][:, 0:TB], [(ones_b, sqb[:, :])], ["sqb", "cstb"], pk(b))
            actf(rq[:, :], ps[b][:, 0:TB], AF.Ln, pk(b) + ["epsT"], ["rq"], bias=epsT[:, 0:1])
            actf(rq[:, :], rq[:, :], AF.Exp, ["rq"], ["rq"], scale=-0.5)
            stt(dest, src, mult, rq[:, :], ALU.mult, ALU.mult, [src_key, "rq"], [dest_key])

        def softplus_small(dst, src_ps, bias_ap, n, keys_in, key_out):
            tt("dve", smt[:, 0:n], src_ps, bias_ap, ALU.add, keys_in, ["smt0"])
            stt(smt[:, 16:16 + n], smt[:, 0:n], -1.0, smt[:, 0:n], ALU.mult, ALU.max, ["smt0"], ["smt1"])
            actf(smt[:, 16:16 + n], smt[:, 16:16 + n], AF.Exp, ["smt1"], ["smt1"], scale=-1.0)
            actf(smt[:, 16:16 + n], smt[:, 16:16 + n], AF.Ln, ["smt1"], ["smt1"], bias=1.0)
            stt(dst, smt[:, 0:n], 0.0, smt[:, 16:16 + n], ALU.max, ALU.add, ["smt0", "smt1"], [key_out])

        def ssd_tile(l, t):
            tsl = slice(t * 128, (t + 1) * 128)
            smk = ("sm", t)
            a_ap = sm_tok[:, t, 16:32]
            b = bankB()
            mm(ps[b][:, 0:16], [(UPI, a_ap)], ["cst", smk], pk(b, [0]))
            mm(ps[b][:, 16:32], [(LOWS, a_ap)], ["cst", smk], pk(b, [0]))
            mm(ps[b][:, 32:48], [(ones_f, a_ap)], ["cst", smk], pk(b, [0]))
            actf(ea[:, :], ps[b][:, 0:48], AF.Exp, pk(b, [0]), ["ea"])
            xs3 = xs_tok[:, t, :].rearrange("p (h d) -> p h d", h=16)
            tt("dve", xdt[:, :, :], xs3, bc(sm_tok[:, t, 0:16].unsqueeze(2), [128, 16, 64]), ALU.mult,
               [("xs_tok", t), smk], ["xdt"])
            tt("pool", xdtd[:, :, :], xdt[:, :, :], bc(ea[:, 16:32].unsqueeze(2), [128, 16, 64]), ALU.mult,
               ["xdt", "ea"], ["xdtd"])
            yb = []
            fb = []
            for g in range(2):
                tt("dve", Gm[:, :, :], bc(UPI.unsqueeze(1), [128, 8, 128]),
                   bc(sm_tok[:, t, 16 + g * 8:24 + g * 8].unsqueeze(2), [128, 8, 128]), ALU.mult,
                   ["cst", smk], ["Gm"])
                sbk = [bankB(), bankB()]
                for hf in range(2):
                    mm(ps[sbk[hf]][:, :], [(LOWS, Gm[:, hf * 4:(hf + 1) * 4, :].rearrange("p h n -> p (h n)"))],
                       ["cst", "Gm"], pk(sbk[hf]))
                    actf(LT[:, hf * 4:(hf + 1) * 4, :].rearrange("p h n -> p (h n)"), ps[sbk[hf]][:, :], AF.Exp,
                         pk(sbk[hf]), [("LT", hf)])
                cbk = bankB()
                mm(ps[cbk][:, 0:128], [(BT[:, g, tsl], CT[:, g, tsl])], [("BT", g), ("CT", g)], pk(cbk, [0]))
                tt("dve", CBm[:, :], ps[cbk][:, 0:128], UPI, ALU.mult, pk(cbk, [0]) + ["cst"], ["CBm"])
                tt("dve", MT[:, :, :], LT[:, :, :], bc(CBm[:, :].unsqueeze(1), [128, 8, 128]), ALU.mult,
                   [("LT", 0), ("LT", 1), "CBm"], ["MT"])
                by = bankB()
                for hh in range(8):
                    mm(ps[by][:, hh * 64:(hh + 1) * 64], [(MT[:, hh, :], xdt[:, g * 8 + hh, :])],
                       ["MT", "xdt"], pk(by, [hh // 2]))
                bf_ = bankB()
                mm(ps[bf_][:, :], [(CT[:, g, tsl], SsB[l][:, g * 512:(g + 1) * 512])],
                   [("CT", g), ("SsB", l)], pk(bf_))
                tf3 = tokf[:, g * 512:(g + 1) * 512].rearrange("p (h d) -> p h d", h=8)
                tt("dve", tf3, ps[bf_][:, :].rearrange("p (h d) -> p h d", h=8),
                   bc(ea[:, g * 8:(g + 1) * 8].unsqueeze(2), [128, 8, 64]), ALU.mult,
                   pk(bf_) + ["ea"], [("tokf", g)])
                tt("dve", tokf[:, g * 512:(g + 1) * 512], ps[by][:, :], tokf[:, g * 512:(g + 1) * 512], ALU.add,
                   pk(by) + [("tokf", g)], [("tokf", g)])
            o = l * BPL
            tt("pool", xsD[:, :].rearrange("p (h d) -> p h d", h=16), xs3,
               bc(bp[:, o + 32:o + 48].unsqueeze(2), [128, 16, 64]), ALU.mult, [("xs_tok", t), "bp"], ["xsD"])
            TK = [("tokf", 0), ("tokf", 1)]
            tt("pool", tokf[:, :], tokf[:, :], xsD[:, :], ALU.add, TK + ["xsD"], TK)
            tt("dve", tokf[:, :], tokf[:, :], zs[:, t, :], ALU.mult, TK + [("zs", t)], TK)
            actf(junk32[:, :], tokf[:, :], AF.Square, TK, ["junk32"])
            P.add("dve", lambda e: e.tensor_reduce(out=ssq[:, 8:10], in_=junk32[:, :].rearrange("p (g d) -> p g d", g=2),
                                                   axis=AX.X, op=ALU.add), ["junk32"], ["ssq_s"])
            rstd_from(ssq[:, 8:10], 2, 1.0 / 512, "ssq_s")
            tt("dve", tokb[:, :].rearrange("p (g d) -> p g d", g=2), tokf[:, :].rearrange("p (g d) -> p g d", g=2),
               bc(rstd[:, 0:2].unsqueeze(2), [128, 2, 512]), ALU.mult, TK + ["rstd"], ["tokb"])
            b2 = bankB()
            pv = ps[b2][:, :].bitcast(BF16)
            for kc in range(8):
                tr(pv[:, kc * 128:(kc + 1) * 128], tokb[:, kc * 128:(kc + 1) * 128], ident_b, ["tokb", "cstb"],
                   pk(b2, [kc // 2]))
            w0 = l * PPL + O_SNW
            tt("dve", yT[:, :, tsl], pv.rearrange("p (k n) -> p k n", k=8),
               bc(pp[:, w0:w0 + 8].unsqueeze(2), [128, 8, 128]), ALU.mult, pk(b2) + ["pp"], [("yT", t)])
            for g in range(2):
                bs_ = bankB()
                mm(ps[bs_][:, :], [(B_tok[:, t, g * 128:(g + 1) * 128],
                                    xdtd[:, g * 8:(g + 1) * 8, :].rearrange("p h d -> p (h d)"))],
                   [("B_tok", t), "xdtd"], pk(bs_))
                s3 = Ss[l][:, g * 512:(g + 1) * 512].rearrange("p (h d) -> p h d", h=8)
                tt("pool", s3, s3, bc(ea[:, 32 + g * 8:40 + g * 8].unsqueeze(2), [128, 8, 64]), ALU.mult,
                   [("Ss", l), "ea"], [("Ss", l)])
                tt("dve", Ss[l][:, g * 512:(g + 1) * 512], ps[bs_][:, :], Ss[l][:, g * 512:(g + 1) * 512], ALU.add,
                   pk(bs_) + [("Ss", l)], [("Ss", l)])
            cp("pool", SsB[l][:, :], Ss[l][:, :], [("Ss", l)], [("SsB", l)])

        def gdn_tile(l, t):
            gst = int(os.environ.get("GST", "99"))
            tsl = slice(t * 128, (t + 1) * 128)
            smk = ("sm", t)
            g_ap = sm_tok[:, t, 32:40]
            beta_ap = sm_tok[:, t, 40:48]
            b = bankB()
            mm(ps[b][:, 0:8], [(UPI, g_ap)], ["cst", smk], pk(b, [0]))
            mm(ps[b][:, 8:16], [(LOWS, g_ap)], ["cst", smk], pk(b, [0]))
            mm(ps[b][:, 16:24], [(ones_f, g_ap)], ["cst", smk], pk(b, [0]))
            actf(eg[:, :], ps[b][:, 0:24], AF.Exp, pk(b, [0]), ["eg"])
            tt("dve", Gm[:, :, :], bc(UPI.unsqueeze(1), [128, 8, 128]), bc(g_ap.unsqueeze(2), [128, 8, 128]),
               ALU.mult, ["cst", smk], ["Gm"])
            sbk = [bankB(), bankB()]
            for hf in range(2):
                mm(ps[sbk[hf]][:, :], [(LOWS, Gm[:, hf * 4:(hf + 1) * 4, :].rearrange("p h n -> p (h n)"))],
                   ["cst", "Gm"], pk(sbk[hf]))
                actf(LT[:, hf * 4:(hf + 1) * 4, :].rearrange("p h n -> p (h n)"), ps[sbk[hf]][:, :], AF.Exp,
                     pk(sbk[hf]), [("LT", hf)])
            tt("dve", LsT[:, :, :], LT[:, :, :], bc(UPS.unsqueeze(1), [128, 8, 128]), ALU.mult,
               [("LT", 0), ("LT", 1), "cst"], ["LsT"])
            tt("dve", LT[:, :, :], LT[:, :, :], bc(UPI.unsqueeze(1), [128, 8, 128]), ALU.mult,
               [("LT", 0), ("LT", 1), "cst"], [("LT", 0), ("LT", 1)])
            k3 = k_tok[:, t, :].rearrange("p (h d) -> p h d", h=8)
            tt("dve", Rk[:, :, :], k3, bc(eg[:, 0:8].unsqueeze(2), [128, 8, 128]), ALU.mult,
               [("k_tok", t), "eg"], ["Rk"])
            tt("dve", kdec[:, :, :], k3, bc(eg[:, 8:16].unsqueeze(2), [128, 8, 128]), ALU.mult,
               [("k_tok", t), "eg"], ["kdec"])
            if gst <= 1:
                return
            for r in range(2):
                bk = bankB()
                bq = bankB()
                for hh in range(4):
                    h = r * 4 + hh
                    mm(ps[bk][:, hh * 128:(hh + 1) * 128], [(kT[:, h, tsl], kT[:, h, tsl])], [("kT", h)],
                       pk(bk, [hh]))
                    mm(ps[bq][:, hh * 128:(hh + 1) * 128], [(kT[:, h, tsl], qT[:, h, tsl])],
                       [("kT", h), ("qT", h)], pk(bq, [hh]))
                for hh in range(4):
                    h = r * 4 + hh
                    stt(XTm[:, h, :], ps[bk][:, hh * 128:(hh + 1) * 128], sm_tok[:, t, 48 + h:49 + h], LsT[:, h, :],
                        ALU.mult, ALU.mult, pk(bk, [hh]) + [smk, "LsT"], [("XT", r)])
                tt("dve", qkmT[:, r * 4:(r + 1) * 4, :].rearrange("p h n -> p (h n)"), ps[bq][:, :],
                   LT[:, r * 4:(r + 1) * 4, :].rearrange("p h n -> p (h n)"), ALU.mult, pk(bq) + [("LT", 0), ("LT", 1)],
                   [("qkmT", r)])
            if gst <= 2:
                return
            for r in range(2):
                bx = bankB()
                for hh in range(4):
                    h = r * 4 + hh
                    mm(ps[bx][:, hh * 128:(hh + 1) * 128], [(XTm[:, h, :], ident_b)], [("XT", r), "cstb"],
                       pk(bx, [hh]))
                actf(Xm[:, r * 4:(r + 1) * 4, :].rearrange("p h n -> p (h n)"), ps[bx][:, :], AF.Copy, pk(bx),
                     [("X", r)])
                tt("dve", PTm[:, r * 4:(r + 1) * 4, :], XTm[:, r * 4:(r + 1) * 4, :],
                   bc(ident_f.unsqueeze(1), [128, 4, 128]), ALU.add, [("XT", r), "cst"], [("PT", r)])
                cp("pool", PTb[:, r * 4:(r + 1) * 4, :], PTm[:, r * 4:(r + 1) * 4, :], [("PT", r)], [("PTb", r)])
            if gst <= 3:
                return
            NL = 6
            for m in range(NL):
                last = (m == NL - 1)
                bxs = []
                for r in range(2):
                    bx = bankB()
                    bxt = None if last else bankB()
                    for hh in range(4):
                        h = r * 4 + hh
                        mm(ps[bx][:, hh * 128:(hh + 1) * 128], [(XTm[:, h, :], Xm[:, h, :])],
                           [("XT", r), ("X", r)], pk(bx, [hh]))
                    if not last:
                        for hh in range(4):
                            h = r * 4 + hh
                            mm(ps[bxt][:, hh * 128:(hh + 1) * 128], [(Xm[:, h, :], XTm[:, h, :])],
                               [("XT", r), ("X", r)], pk(bxt, [hh]))
                    bxs.append((bx, bxt))
                for r in range(2):
                    bx, bxt = bxs[r]
                    actf(Xm[:, r * 4:(r + 1) * 4, :].rearrange("p h n -> p (h n)"), ps[bx][:, :], AF.Copy, pk(bx),
                         [("X", r)])
                    if not last:
                        actf(XTm[:, r * 4:(r + 1) * 4, :].rearrange("p h n -> p (h n)"), ps[bxt][:, :], AF.Copy,
                             pk(bxt), [("XT", r)])
                for r in range(2):
                    bp_ = bankB()
                    for hh in range(4):
                        h = r * 4 + hh
                        mm(ps[bp_][:, hh * 128:(hh + 1) * 128], [(Xm[:, h, :], PTb[:, h, :])],
                           [("X", r), ("PTb", r)], pk(bp_, [hh]))
                    tt("dve", PTm[:, r * 4:(r + 1) * 4, :].rearrange("p h n -> p (h n)"), ps[bp_][:, :],
                       PTm[:, r * 4:(r + 1) * 4, :].rearrange("p h n -> p (h n)"), ALU.add,
                       pk(bp_) + [("PT", r)], [("PT", r)])
                    cp("pool", PTb[:, r * 4:(r + 1) * 4, :], PTm[:, r * 4:(r + 1) * 4, :], [("PT", r)], [("PTb", r)])
            if gst <= 4:
                return
            PTBK = [("PTb", 0), ("PTb", 1)]
            for r in range(2):
                bz = bankB()
                for hh in range(4):
                    h = r * 4 + hh
                    mm(ps[bz][:, hh * 128:(hh + 1) * 128], [(Rk[:, h, :], PTb[:, h, :])], ["Rk"] + PTBK,
                       pk(bz, [hh]))
                actf(nZkT[:, r * 4:(r + 1) * 4, :].rearrange("p h n -> p (h n)"), ps[bz][:, :], AF.Copy, pk(bz),
                     [("nZkT", r)], scale=-1.0)
            for r in range(2):
                bv = bankB()
                for hh in range(4):
                    h = r * 4 + hh
                    mm(ps[bv][:, hh * 128:(hh + 1) * 128],
                       [(PTb[:, h, :], v_tok[:, t, h * 128:(h + 1) * 128]), (nZkT[:, h, :], SgB[l][:, h, :])],
                       PTBK + [("v_tok", t), ("nZkT", r), ("SgB", l)], pk(bv, [hh]))
                for hh in range(4):
                    h = r * 4 + hh
                    actf(vnew[:, h, :], ps[bv][:, hh * 128:(hh + 1) * 128], AF.Identity, pk(bv, [hh]) + [smk],
                         [("vnew", r)], scale=sm_tok[:, t, 40 + h:41 + h])
            if gst <= 5:
                return
            for r in range(2):
                bo = bankB()
                bo2 = bankB()
                for hh in range(4):
                    h = r * 4 + hh
                    mm(ps[bo][:, hh * 128:(hh + 1) * 128], [(qT[:, h, tsl], SgB[l][:, h, :])],
                       [("qT", h), ("SgB", l)], pk(bo, [hh]))
                    mm(ps[bo2][:, hh * 128:(hh + 1) * 128], [(qkmT[:, h, :], vnew[:, h, :])],
                       [("qkmT", r), ("vnew", r)], pk(bo2, [hh]))
                for hh in range(4):
                    h = r * 4 + hh
                    actf(tokf[:, h * 128:(h + 1) * 128], ps[bo][:, hh * 128:(hh + 1) * 128], AF.Identity,
                         pk(bo, [hh]) + ["eg"], [("tokf", r)], scale=eg[:, h:h + 1])
                tt("dve", tokf[:, r * 512:(r + 1) * 512], ps[bo2][:, :], tokf[:, r * 512:(r + 1) * 512], ALU.add,
                   pk(bo2) + [("tokf", r)], [("tokf", r)])
            for r in range(2):
                bs_ = bankB()
                for hh in range(4):
                    h = r * 4 + hh
                    mm(ps[bs_][:, hh * 128:(hh + 1) * 128], [(kdec[:, h, :], vnew[:, h, :])],
                       ["kdec", ("vnew", r)], pk(bs_, [hh]))
                for hh in range(4):
                    h = r * 4 + hh
                    stt(Sg[l][:, h, :], Sg[l][:, h, :], eg[:, 16 + h:17 + h], ps[bs_][:, hh * 128:(hh + 1) * 128],
                        ALU.mult, ALU.add, pk(bs_, [hh]) + [("Sg", l), "eg"], [("Sg", l)])
            cp("pool", SgB[l][:, :, :], Sg[l][:, :, :], [("Sg", l)], [("SgB", l)])
            if gst <= 6:
                return
            actf(junk32[:, :], tokf[:, :], AF.Square, [("tokf", 0), ("tokf", 1)], ["junk32"])
            P.add("dve", lambda e: e.tensor_reduce(out=ssq[:, 8:16], in_=junk32[:, :].rearrange("p (g d) -> p g d", g=8),
                                                   axis=AX.X, op=ALU.add), ["junk32"], ["ssq_s"])
            rstd_from(ssq[:, 8:16], 8, 1.0 / 128, "ssq_s")
            tt("dve", tokf[:, :].rearrange("p (g d) -> p g d", g=8), tokf[:, :].rearrange("p (g d) -> p g d", g=8),
               bc(rstd[:, 0:8].unsqueeze(2), [128, 8, 128]), ALU.mult, [("tokf", 0), ("tokf", 1), "rstd"],
               [("tokf", 0), ("tokf", 1)])
            tt("dve", tokb[:, :], tokf[:, :], zg[:, t, :], ALU.mult, [("tokf", 0), ("tokf", 1), ("zg", t)], ["tokb"])
            b2 = bankB()
            pv = ps[b2][:, :].bitcast(BF16)
            for kc in range(8):
                tr(pv[:, kc * 128:(kc + 1) * 128], tokb[:, kc * 128:(kc + 1) * 128], ident_b, ["tokb", "cstb"],
                   pk(b2, [kc // 2]))
            w0 = l * PPL + O_GNW
            actf(oT[:, :, tsl], pv.rearrange("p (k n) -> p k n", k=8), AF.Identity, pk(b2) + ["pp"], [("oT", t)],
                 scale=pp[:, w0:w0 + 1])

        def mixer(l):
            rmsnorm_T(l, O_NMIX)
            if stage <= 0:
                return
            for c in range(2):
                bi, = wuse(1, [(l, c)])
                dense_tok(bi, 512, lambda t, b, c=c: actf(zs[:, t, c * 512:(c + 1) * 512], ps[b][:, :], AF.Silu,
                                                          pk(b), [("zs", t)]))
            if stage <= 1:
                return
            for c in range(3):
                bi, = wuse(1, [(l, 2 + c)])
                for f in range(4):
                    cc = c * 4 + f
                    b = dense_feat(bi, f, hT, hT_keys)
                    bias_ap = pp[:, l * PPL + O_SCB + cc:l * PPL + O_SCB + cc + 1]
                    if cc < 8:
                        i = ctr["u"] % 2
                        conv_silu(l, b, cc, O_SCW + cc * 4, bias_ap, ftmp[i][:, :], [("ftmp", i)])
                        to_tok(ftmp[i], ("ftmp", i), lambda t, cc=cc: xs_tok[:, t, cc * 128:(cc + 1) * 128],
                               [("xs_tok", t) for t in range(NT)])
                    elif cc < 10:
                        g = cc - 8
                        conv_silu(l, b, cc, O_SCW + cc * 4, bias_ap, BT[:, g, :], [("BT", g)])
                        to_tok(BT[:, g, :], ("BT", g), lambda t, g=g: B_tok[:, t, g * 128:(g + 1) * 128],
                               [("B_tok", t) for t in range(NT)])
                    else:
                        g = cc - 10
                        conv_silu(l, b, cc, O_SCW + cc * 4, bias_ap, CT[:, g, :], [("CT", g)])
            if stage <= 2:
                return
            bi, = wuse(1, [(l, 5)])
            o = l * BPL

            def small_evac(t, b):
                kin = pk(b) + ["bp"]
                smk = ("sm", t)
                softplus_small(sm_tok[:, t, 0:16], ps[b][:, 0:16], bp[:, o:o + 16], 16, kin, smk)
                tt("dve", sm_tok[:, t, 16:32], sm_tok[:, t, 0:16], nexpA[:, l * 24:l * 24 + 16], ALU.mult,
                   [smk, "nexpA"], [smk])
                softplus_small(sm_tok[:, t, 32:40], ps[b][:, 16:24], bp[:, o + 56:o + 64], 8, kin + [smk], smk)
                tt("dve", sm_tok[:, t, 32:40], sm_tok[:, t, 32:40], nexpA[:, l * 24 + 16:l * 24 + 24], ALU.mult,
                   [smk, "nexpA"], [smk])
                actf(sm_tok[:, t, 40:48], ps[b][:, 24:32], AF.Sigmoid, pk(b) + [smk], [smk])
                ts("dve", sm_tok[:, t, 48:56], sm_tok[:, t, 40:48], -1.0, ALU.mult, [smk], [smk])
            dense_tok(bi, 32, small_evac)
            if stage <= 3:
                return
            for t in range(NT):
                ssd_tile(l, t)
            if stage <= 4:
                return
            for c in range(6):
                bi, = wuse(1, [(l, 6 + c)])
                for f in range(4):
                    cc = c * 4 + f
                    b = dense_feat(bi, f, hT, hT_keys)
                    conv_silu(l, b, 12 + cc, O_GCW + cc * 4, None, ftq[f][:, :], [("ftq", f)])
                for f in range(4):
                    cc = c * 4 + f
                    h = cc % 8
                    if cc < 8:
                        l2norm_feat(ftq[f][:, :], ("ftq", f), qT[:, h, :], ("qT", h), 128.0 ** -0.5)
                    elif cc < 16:
                        l2norm_feat(ftq[f][:, :], ("ftq", f), kT[:, h, :], ("kT", h), 1.0)
                        to_tok(kT[:, h, :], ("kT", h), lambda t, h=h: k_tok[:, t, h * 128:(h + 1) * 128],
                               [("k_tok", t) for t in range(NT)])
                    else:
                        to_tok(ftq[f], ("ftq", f), lambda t, h=h: v_tok[:, t, h * 128:(h + 1) * 128],
                               [("v_tok", t) for t in range(NT)])
            for c in range(2):
                bi, = wuse(1, [(l, 12 + c)])
                dense_tok(bi, 512, lambda t, b, c=c: actf(zg[:, t, c * 512:(c + 1) * 512], ps[b][:, :], AF.Silu,
                                                          pk(b), [("zg", t)]))
            if stage <= 5:
                return
            for t in range(NT):
                gdn_tile(l, t)
            if stage <= 6:
                return
            for c in range(4):
                bi, = wuse(1, [(l, 14 + c)])
                for f in range(4):
                    j = c * 4 + f
                    b = dense_feat(bi, f, hT, hT_keys)
                    actf(gT[:, j, :], ps[b][:, 0:TB], AF.Sigmoid, pk(b), [("gT", j)])
            yT_keys = [("yT", t) for t in range(NT)]
            oT_keys = [("oT", t) for t in range(NT)]
            for c in range(2):
                b1, b2 = wuse(2, [(l, 18 + c), (l, 20 + c)])
                for f in range(4):
                    j = c * 4 + f
                    pa = dense_feat(b1, f, yT, yT_keys)
                    pb_ = dense_feat(b2, f, oT, oT_keys)
                    i = ctr["u"] % 2
                    ctr["u"] += 1
                    tt("dve", mtmp[i][:, 0, :], ps[pa][:, 0:TB], gT[:, j, :], ALU.mult, pk(pa) + [("gT", j)],
                       [("mtmp", i, 0)])
                    tt("dve", mtmp[i][:, 1, :], ps[pb_][:, 0:TB], gT[:, 8 + j, :], ALU.mult,
                       pk(pb_) + [("gT", 8 + j)], [("mtmp", i, 1)])
                    tt("pool", hT[:, j, :], mtmp[i][:, 0, :], mtmp[i][:, 1, :], ALU.add,
                       [("mtmp", i, 0), ("mtmp", i, 1)], hT_keys)
            if stage <= 7:
                return
            for c in range(2):
                bi, = wuse(1, [(l, 22 + c)])
                dense_tok(bi, 512, lambda t, b, c=c: tt("dve", xb[:, t, c * 512:(c + 1) * 512], ps[b][:, :],
                                                        xb[:, t, c * 512:(c + 1) * 512], ALU.add,
                                                        pk(b) + [("xb", t)], [("xb", t)]))

        def ffn(l):
            rmsnorm_T(l, O_NFFN)
            for q in range(6):
                nf = 4 if q < 5 else 2
                bg, bu = wuse(2, [(l, 24 + 2 * q), (l, 25 + 2 * q)])
                for f in range(nf):
                    j = q * 4 + f
                    pg = dense_feat(bg, f, hT, hT_keys)
                    pu = dense_feat(bu, f, hT, hT_keys)
                    i = ctr["u"] % 2
                    ctr["u"] += 1
                    actf(cacc[i][:, :], ps[pg][:, 0:TB], AF.Silu, pk(pg), [("cacc", i)])
                    tt("dve", actT[:, j, :], ps[pu][:, 0:TB], cacc[i][:, :], ALU.mult, pk(pu) + [("cacc", i)],
                       [("actT", j)])
            for c in range(2):
                banks = [bankA() for _ in range(NT)]
                for kg in range(3):
                    bi, = wuse(1, [(l, 36 + c * 3 + kg)])
                    nk = 8 if kg < 2 else 6
                    for t in range(NT):
                        def fn(e, t=t, kg=kg, nk=nk, bi=bi, banks=banks):
                            ins = None
                            for kk in range(nk):
                                kc = kg * 8 + kk
                                ins = e.matmul(ps[banks[t]][:, :], actT[:, kc, t * 128:(t + 1) * 128],
                                               wb[bi][:, kk, :], start=(kc == 0), stop=(kc == 21))
                            return ins
                        P.add("pe", fn, [("actT", kg * 8 + kk) for kk in range(nk)] + [("wb", bi)], pk(banks[t]))
                for t in range(NT):
                    tt("dve", xb[:, t, c * 512:(c + 1) * 512], ps[banks[t]][:, :], xb[:, t, c * 512:(c + 1) * 512],
                       ALU.add, pk(banks[t]) + [("xb", t)], [("xb", t)])

        def dump(nm, src_fn):
            if nm in dbg_d:
                for t in range(NT):
                    P.dma(dbg_d[nm][t * 128:(t + 1) * 128, :], src_fn(t), [("xb", t)], [("dbg", nm, t)], "dbg%d" % t)

        fin_ops = []
        for blk in range(NB):
            r0 = blk * TB
            P.dma(xb[:, :, :], x_in[r0:r0 + TB, :].rearrange("(t p) d -> p t d", p=128), [],
                  [("xb", t) for t in range(NT)], "xb")
            for l in range(depth):
                mixer(l)
                if dbgsb and blk == 0 and l == 0:
                    TT = range(NT)
                    for nm_, ten, kk_ in (
                            ("sm_tok", sm_tok, [("sm", t) for t in TT]), ("zs", zs, [("zs", t) for t in TT]),
                            ("xs_tok", xs_tok, [("xs_tok", t) for t in TT]), ("BT", BT, [("BT", g) for g in range(2)]),
                            ("CT", CT, [("CT", g) for g in range(2)]), ("B_tok", B_tok, [("B_tok", t) for t in TT]),
                            ("yT", yT, [("yT", t) for t in TT]), ("qT", qT, [("qT", h) for h in range(8)]),
                            ("kT", kT, [("kT", h) for h in range(8)]), ("k_tok", k_tok, [("k_tok", t) for t in TT]),
                            ("v_tok", v_tok, [("v_tok", t) for t in TT]), ("zg", zg, [("zg", t) for t in TT]),
                            ("oT", oT, [("oT", t) for t in TT]), ("gT", gT, [("gT", j) for j in range(16)]),
                            ("hT", hT, [("hT", t) for t in TT]), ("Ss0", Ss[0], [("Ss", 0)]), ("Sg0", Sg[0], [("Sg", 0)]),
                            ("ea", ea, ["ea"]), ("eg", eg, ["eg"]), ("PTm", PTm, [("PT", 0), ("PT", 1)]),
                            ("Xm", Xm, [("X", 0), ("X", 1)]), ("tokf", tokf, [("tokf", 0), ("tokf", 1)]),
                            ("vnew", vnew, [("vnew", 0), ("vnew", 1)]), ("LT", LT, [("LT", 0), ("LT", 1)])):
                        shp = list(ten.shape)
                        fr = int(np.prod(shp[1:]))
                        dd = nc.dram_tensor("sb_" + nm_, [128, fr], ten.dtype, kind="ExternalOutput").ap()
                        src = ten[:, :] if len(shp) == 2 else ten[:, :, :].rearrange("p a b -> p (a b)")
                        P.dma(dd, src, kk_, [("sbdump", nm_)], "d_" + nm_)
                if blk == 0:
                    dump("mix%d" % l, lambda t: xb[:, t, :])
                if stage >= 100:
                    ffn(l)
                if blk == 0:
                    dump("ffn%d" % l, lambda t: xb[:, t, :])
            for t in range(NT):
                actf(junk[:, :], xb[:, t, :], AF.Square, [("xb", t)], ["junk", ("ssq", t)], accum=ssq[:, t:t + 1])
            actf(lnv[:, 0:NT], ssq[:, 0:NT], AF.Ln, [("ssq", t) for t in range(NT)] + ["epsT"], ["lnv"],
                 scale=1.0 / D, bias=epsT[:, 0:1])
            actf(rstd[:, 0:NT], lnv[:, 0:NT], AF.Exp, ["lnv"], ["rstd"], scale=-0.5)
            for t in range(NT):
                stt(xb[:, t, :], xb[:, t, :], rstd[:, t:t + 1], bp[:, depth * BPL:depth * BPL + D], ALU.mult, ALU.mult,
                    [("xb", t), "rstd", "bp"], [("xb", t)])
            fin_ops.append(P.dma(out_d[r0:r0 + TB, :].rearrange("(t p) d -> p t d", p=128), xb[:, :, :],
                                 [("xb", t) for t in range(NT)], [("out", blk)], "out"))
        P.emit([fin_ops[-1]])
    return nc


def host_params(inp, depth=DEPTH):
    f = np.float32
    pp = np.zeros((128, depth * PPL), f)
    bp = np.zeros((128, depth * BPL + D), f)

    def fm(v):
        return np.ascontiguousarray(np.asarray(v, f).reshape(-1, 128).T)
    for l in range(depth):
        o = l * PPL
        pp[:, o + O_NMIX:o + O_NMIX + 8] = fm(inp["norm_mix_w"][l])
        pp[:, o + O_NFFN:o + O_NFFN + 8] = fm(inp["norm_ffn_w"][l])
        pp[:, o + O_SNW:o + O_SNW + 8] = fm(inp["ssm_norm_w"][l])
        pp[:, o + O_GNW:o + O_GNW + 1] = fm(inp["gdn_norm_w"][l])
        scw = np.asarray(inp["ssm_conv_w"][l], f)
        pp[:, o + O_SCW:o + O_SCW + 48] = scw.reshape(4, 12, 128).transpose(2, 1, 0).reshape(128, 48)
        pp[:, o + O_SCB:o + O_SCB + 12] = fm(inp["ssm_conv_b"][l])
        gcw = np.asarray(inp["gdn_conv_w"][l], f)
        pp[:, o + O_GCW:o + O_GCW + 96] = gcw.reshape(4, 24, 128).transpose(2, 1, 0).reshape(128, 96)
        b = l * BPL
        bp[:, b:b + 16] = np.asarray(inp["ssm_dt_bias"][l], f)[None, :]
        bp[:, b + 16:b + 32] = np.asarray(inp["ssm_a_log"][l], f)[None, :]
        bp[:, b + 32:b + 48] = np.asarray(inp["ssm_d"][l], f)[None, :]
        bp[:, b + 48:b + 56] = np.asarray(inp["gdn_a_log"][l], f)[None, :]
        bp[:, b + 56:b + 64] = np.asarray(inp["gdn_dt_bias"][l], f)[None, :]
    bp[:, depth * BPL:] = np.asarray(inp["final_norm_w"], f)[None, :]
    r = np.arange(128)
    cst = np.concatenate([np.eye(128), (r[:, None] > r[None, :]), (r[:, None] <= r[None, :]),
                          (r[None, :] > r[:, None]), np.ones((128, 128))], axis=1).astype(f)
    return pp, bp, cst


_NC_CACHE = {}


def kernel(**inputs):
    inp = {k: np.asarray(v) for k, v in inputs.items()}
    x = np.ascontiguousarray(inp["x"], dtype=np.float32)
    pp, bp, cst = host_params(inp)
    if "nc" not in _NC_CACHE:
        _NC_CACHE["nc"] = build()
    nc = _NC_CACHE["nc"]
    shared = {k: np.ascontiguousarray(inp[k], dtype=np.float32) for k in
              ("w_in", "w_proj_ssm", "w_proj_gdn", "w_out", "w_ffn_in", "w_ffn_down")}
    in_maps = []
    for c in range(BATCH):
        m = dict(shared)
        m["x"] = x[c]
        m["pp"] = pp
        m["bp"] = bp
        m["cst"] = cst
        in_maps.append(m)
    res = run_bass_kernel_spmd(nc, in_maps, core_ids=list(range(BATCH)))
    return np.stack([np.asarray(r["out"], dtype=np.float32) for r in res.results], axis=0)
```
